# Optimizing a Trainium2 kernel written in Bass

```python
import math
import jax, jax.numpy as jnp
from jax import lax
import numpy as np

D_MODEL = 4096
BATCH = 2
SEQ = 8192
DEPTH = 2

MIX_WIDTH = D_MODEL
ATTN_WIDTH = MIX_WIDTH // 2
CONV_WIDTH = MIX_WIDTH - ATTN_WIDTH
ATTN_HEAD_DIM = 128
N_ATTN_HEADS = ATTN_WIDTH // (2 * ATTN_HEAD_DIM)
QK_WIDTH = N_ATTN_HEADS * 2 * ATTN_HEAD_DIM
W_IN_EVEN = 2 * QK_WIDTH + ATTN_WIDTH + 2 * CONV_WIDTH
CONF_KERNEL = 31
SHORT_KERNEL = 3
D_FF = 256 * ((8 * D_MODEL // 3 + 255) // 256)
FFN_KERNEL = 3
N_BUCKETS = 32
MAX_DISTANCE = 128
Q_BLOCK = 128
EPS = 1e-6
N_EVEN = (DEPTH + 1) // 2
N_ODD = DEPTH // 2

kernel_name = 'hybrid_diffattn_conformer_shortconv_encoder'


def rms_norm(x, g):
    xf = x.astype(jnp.float32)
    y = xf * lax.rsqrt(jnp.mean(xf * xf, axis=-1, keepdims=True) + EPS)
    return (y * g.astype(jnp.float32)).astype(x.dtype)


def layer_norm(x, g, b):
    xf = x.astype(jnp.float32)
    mu = jnp.mean(xf, axis=-1, keepdims=True)
    var = jnp.mean(jnp.square(xf - mu), axis=-1, keepdims=True)
    y = (xf - mu) * lax.rsqrt(var + EPS)
    return (y * g.astype(jnp.float32) + b.astype(jnp.float32)).astype(x.dtype)


def depthwise_conv(x, w):
    k, c = w.shape
    pad = (k - 1) // 2
    return lax.conv_general_dilated(
        x, w[:, None, :], window_strides=(1,), padding=[(pad, pad)],
        dimension_numbers=('NWC', 'WIO', 'NWC'), feature_group_count=c)


def t5_bucket(rel):
    nb = N_BUCKETS // 2
    max_exact = nb // 2
    ret = jnp.where(rel > 0, nb, 0)
    n = jnp.abs(rel)
    nf = jnp.maximum(n, max_exact).astype(jnp.float32)
    large = max_exact + (jnp.log(nf / max_exact) / math.log(MAX_DISTANCE / max_exact)
                         * (nb - max_exact)).astype(jnp.int32)
    large = jnp.minimum(large, nb - 1)
    return ret + jnp.where(n < max_exact, n, large)


def diff_attention(q, k, v, rel_bias, lam, lam_init, subln_g):
    b, s, h, _, d = q.shape
    scale = d ** -0.5
    nblk = s // Q_BLOCK
    qb = q.reshape(b, nblk, Q_BLOCK, h, 2, d).transpose(1, 0, 2, 3, 4, 5)
    kpos = jnp.arange(s, dtype=jnp.int32)

    def block(args):
        q_blk, start = args
        qpos = start + jnp.arange(Q_BLOCK, dtype=jnp.int32)
        bias = rel_bias[t5_bucket(kpos[None, :] - qpos[:, None])]
        bias = bias.transpose(2, 0, 1).astype(jnp.float32)
        logits = jnp.einsum('bqhmd,bkhmd->bhmqk', q_blk, k,
                            preferred_element_type=jnp.float32) * scale
        p = jax.nn.softmax(logits + bias[None, :, None], axis=-1)
        w = p[:, :, 0] - lam * p[:, :, 1]
        return jnp.einsum('bhqk,bkhe->bqhe', w.astype(v.dtype), v)

    starts = jnp.arange(nblk, dtype=jnp.int32) * Q_BLOCK
    out = lax.map(block, (qb, starts))
    out = out.transpose(1, 0, 2, 3, 4).reshape(b, s, h, 2 * d)
    out = rms_norm(out, subln_g) * (1.0 - lam_init)
    return out.reshape(b, s, h * 2 * d)


def even_mixer(hn, w_in, lq1, lk1, lq2, lk2, subln_g, rel_bias,
               conf_w, conf_b, conf_ln_g, conf_ln_b, w_out, lam_init):
    b, s, _ = hn.shape
    proj = hn @ w_in
    q, k, v, cv, cg = jnp.split(
        proj, [QK_WIDTH, 2 * QK_WIDTH, 2 * QK_WIDTH + ATTN_WIDTH,
               2 * QK_WIDTH + ATTN_WIDTH + CONV_WIDTH], axis=-1)
    q = q.reshape(b, s, N_ATTN_HEADS, 2, ATTN_HEAD_DIM)
    k = k.reshape(b, s, N_ATTN_HEADS, 2, ATTN_HEAD_DIM)
    v = v.reshape(b, s, N_ATTN_HEADS, 2 * ATTN_HEAD_DIM)
    lam = (jnp.exp(jnp.sum(lq1.astype(jnp.float32) * lk1.astype(jnp.float32)))
           - jnp.exp(jnp.sum(lq2.astype(jnp.float32) * lk2.astype(jnp.float32)))
           + lam_init)
    attn = diff_attention(q, k, v, rel_bias, lam, lam_init, subln_g)
    u = cv * jax.nn.sigmoid(cg)
    u = depthwise_conv(u, conf_w) + conf_b
    u = jax.nn.silu(layer_norm(u, conf_ln_g, conf_ln_b))
    return jnp.concatenate([attn, u], axis=-1) @ w_out


def odd_mixer(hn, w_in, conv_w, w_out):
    g_b, g_c, xv = jnp.split(hn @ w_in, 3, axis=-1)
    y = g_b * depthwise_conv(g_c * xv, conv_w)
    return y @ w_out


def conv_ffn(hn, w_gate, w_up, conv_w, conv_b, w_down):
    g = depthwise_conv(hn @ w_gate, conv_w) + conv_b
    return (jax.nn.gelu(g, approximate=True) * (hn @ w_up)) @ w_down


def setup_inputs(seed: int = 0) -> dict:
    key = jax.random.key(seed)
    ks = jax.random.split(key, 26)
    f32 = jnp.float32

    def nrm(k, shape, scale):
        return jax.random.normal(k, shape, f32) * scale

    def gain(k, shape):
        return 1.0 + 0.05 * jax.random.normal(k, shape, f32)

    return {
        'x': nrm(ks[0], (BATCH, SEQ, D_MODEL), 1.0),
        'rel_bias': nrm(ks[1], (N_BUCKETS, N_ATTN_HEADS), 0.5),
        'ev_w_in': nrm(ks[2], (N_EVEN, D_MODEL, W_IN_EVEN), D_MODEL ** -0.5),
        'ev_lambda_q1': nrm(ks[3], (N_EVEN, ATTN_HEAD_DIM), 0.1),
        'ev_lambda_k1': nrm(ks[4], (N_EVEN, ATTN_HEAD_DIM), 0.1),
        'ev_lambda_q2': nrm(ks[5], (N_EVEN, ATTN_HEAD_DIM), 0.1),
        'ev_lambda_k2': nrm(ks[6], (N_EVEN, ATTN_HEAD_DIM), 0.1),
        'ev_subln_g': gain(ks[7], (N_EVEN, 2 * ATTN_HEAD_DIM)),
        'ev_conf_w': nrm(ks[8], (N_EVEN, CONF_KERNEL, CONV_WIDTH), CONF_KERNEL ** -0.5),
        'ev_conf_b': nrm(ks[9], (N_EVEN, CONV_WIDTH), 0.02),
        'ev_conf_ln_g': gain(ks[10], (N_EVEN, CONV_WIDTH)),
        'ev_conf_ln_b': nrm(ks[11], (N_EVEN, CONV_WIDTH), 0.02),
        'ev_w_out': nrm(ks[12], (N_EVEN, MIX_WIDTH, D_MODEL), MIX_WIDTH ** -0.5),
        'od_w_in': nrm(ks[13], (N_ODD, D_MODEL, 3 * D_MODEL), D_MODEL ** -0.5),
        'od_conv_w': nrm(ks[14], (N_ODD, SHORT_KERNEL, D_MODEL), SHORT_KERNEL ** -0.5),
        'od_w_out': nrm(ks[15], (N_ODD, D_MODEL, D_MODEL), D_MODEL ** -0.5),
        'ffn_w_gate': nrm(ks[16], (DEPTH, D_MODEL, D_FF), D_MODEL ** -0.5),
        'ffn_w_up': nrm(ks[17], (DEPTH, D_MODEL, D_FF), D_MODEL ** -0.5),
        'ffn_conv_w': nrm(ks[18], (DEPTH, FFN_KERNEL, D_FF), FFN_KERNEL ** -0.5),
        'ffn_conv_b': nrm(ks[19], (DEPTH, D_FF), 0.02),
        'ffn_w_down': nrm(ks[20], (DEPTH, D_FF, D_MODEL), D_FF ** -0.5),
        'pre_mix_g': gain(ks[21], (DEPTH, D_MODEL)),
        'post_mix_g': gain(ks[22], (DEPTH, D_MODEL)),
        'pre_ffn_g': gain(ks[23], (DEPTH, D_MODEL)),
        'post_ffn_g': gain(ks[24], (DEPTH, D_MODEL)),
    }


def reference(x, rel_bias, ev_w_in, ev_lambda_q1, ev_lambda_k1, ev_lambda_q2, ev_lambda_k2,
              ev_subln_g, ev_conf_w, ev_conf_b, ev_conf_ln_g, ev_conf_ln_b, ev_w_out,
              od_w_in, od_conv_w, od_w_out, ffn_w_gate, ffn_w_up, ffn_conv_w, ffn_conv_b,
              ffn_w_down, pre_mix_g, post_mix_g, pre_ffn_g, post_ffn_g):
    for i in range(DEPTH):
        j = i // 2
        hn = rms_norm(x, pre_mix_g[i])
        if i % 2 == 0:
            lam_init = 0.8 - 0.6 * math.exp(-0.3 * i)
            m = even_mixer(hn, ev_w_in[j], ev_lambda_q1[j], ev_lambda_k1[j],
                           ev_lambda_q2[j], ev_lambda_k2[j], ev_subln_g[j], rel_bias,
                           ev_conf_w[j], ev_conf_b[j], ev_conf_ln_g[j], ev_conf_ln_b[j],
                           ev_w_out[j], lam_init)
        else:
            m = odd_mixer(hn, od_w_in[j], od_conv_w[j], od_w_out[j])
        x = x + rms_norm(m, post_mix_g[i])
        hn = rms_norm(x, pre_ffn_g[i])
        f = conv_ffn(hn, ffn_w_gate[i], ffn_w_up[i], ffn_conv_w[i], ffn_conv_b[i], ffn_w_down[i])
        x = x + rms_norm(f, post_ffn_g[i])
    return x
```

```python
import contextlib
import math
import numpy as np
import concourse.bass as bass
import concourse.mybir as mybir
from concourse.bass_utils import run_bass_kernel_spmd

F32 = mybir.dt.float32
BF16 = mybir.dt.bfloat16
ALU = mybir.AluOpType
AF = mybir.ActivationFunctionType
ENGINES = ("sync", "gpsimd", "scalar", "vector", "tensor")
EPS = 1e-6
PH_LIMIT = [999]
ATT_CUT = [9]
N_BUCKETS = 32
MAX_DISTANCE = 128
CONF_K = 31


class Buf:
    __slots__ = ("name", "t", "last_w", "readers", "dsem")

    def __init__(self, name, t=None):
        self.name = name
        self.t = t
        self.last_w = None
        self.readers = []
        self.dsem = None


def I(m, **kw):
    return (m, kw)


class Phase:
    def __init__(self, nc, name):
        self.nc = nc
        self.name = name
        self.stack = contextlib.ExitStack()
        self.cm = nc.cleanup_on_exit()
        self.cm.__enter__()
        self.ops = {e: [] for e in ENGINES}
        self.sems = {}
        self.counts = {}
        self.nbuf = 0
        self.rr = 0
        self.bg = []

    def sem(self, key):
        if key not in self.sems:
            self.sems[key] = self.nc.alloc_semaphore(f"{self.name}_{key}")
            self.counts[key] = 0
        return self.sems[key]

    def sbuf(self, name, shape, dtype):
        t = self.stack.enter_context(self.nc.sbuf_tensor(f"{self.name}_{name}", list(shape), dtype))
        return Buf(name, t)

    def psum(self, name, shape, dtype=F32):
        t = self.stack.enter_context(self.nc.psum_tensor(f"{self.name}_{name}", list(shape), dtype))
        return Buf(name, t)

    def _deps(self, reads, writes):
        waits = []
        for b in reads:
            if b.last_w is not None:
                waits.append(b.last_w)
        for b in writes:
            if b.last_w is not None:
                waits.append(b.last_w)
            waits.extend(b.readers)
        return waits

    def _commit(self, tok, reads, writes):
        for b in reads:
            b.readers.append(tok)
        for b in writes:
            b.last_w = tok
            b.readers = []

    def op(self, eng, fn, reads=(), writes=()):
        if isinstance(fn, tuple):
            fn = [fn]
        waits = self._deps(reads, writes)
        if eng == "tensor":
            waits = [w_ for w_ in waits if w_[0] != "e_tensor"]
        key = "e_" + eng
        self.sem(key)
        self.counts[key] += 1
        tok = (key, self.counts[key])
        self.ops[eng].append((waits, fn, key, 1))
        self._commit(tok, reads, writes)
        return tok

    def dma(self, eng, out, in_, reads=(), writes=()):
        waits = self._deps(reads, writes)
        bufs = list(writes) + list(reads)
        b = bufs[0]
        if b.dsem is None:
            self.nbuf += 1
            b.dsem = f"d{self.nbuf}"
        key = b.dsem
        self.sem(key)
        self.counts[key] += 16
        tok = (key, self.counts[key])

        self.ops[eng].append((waits, [("dma_start", dict(out=out, in_=in_))], key, 16))
        self._commit(tok, reads, writes)
        return tok

    def load(self, out, in_, writes):
        return self.dma("sync", out, in_, writes=writes)

    def store(self, out, in_, reads):
        return self.dma("gpsimd", out, in_, reads=reads)

    def add_bg(self, items):
        self.bg.extend(items)

    def _merge_bg(self):
        if not self.bg:
            return
        self.sem("bg")
        g = self.ops["gpsimd"]
        n, m = len(g), len(self.bg)
        pos = [bi * n // m for bi in range(m)]
        merged = []
        bi = 0
        for i in range(n + 1):
            while bi < m and pos[bi] <= i:
                out, in_ = self.bg[bi]
                self.counts["bg"] += 16
                merged.append(([], [("dma_start", dict(out=out, in_=in_))], "bg", 16))
                bi += 1
            if i < n:
                merged.append(g[i])
        self.ops["gpsimd"] = merged

    def run(self):
        nc = self.nc
        self._merge_bg()
        final_dma = {k: v for k, v in self.counts.items() if not k.startswith("e_")}
        with nc.Block() as block:
            for eng in ENGINES:
                oplist = self.ops[eng]
                if not oplist and eng != "gpsimd":
                    continue

                def body(e, oplist=oplist, eng=eng):
                    waited = {}
                    for waits, fn, key, inc in oplist:
                        for (wk, wv) in waits:
                            if waited.get(wk, 0) >= wv:
                                continue
                            e.wait_ge(self.sems[wk], wv)
                            waited[wk] = wv
                        for (mname, kw) in fn:
                            ins = getattr(e, mname)(**kw)
                        ins.then_inc(self.sems[key], inc)
                    if eng == "gpsimd":
                        for k, v in final_dma.items():
                            if v > 0 and waited.get(k, 0) < v:
                                e.wait_ge(self.sems[k], v)
                        for k in ("e_sync", "e_scalar", "e_vector", "e_tensor"):
                            if self.counts.get(k, 0) > 0:
                                e.wait_ge(self.sems[k], self.counts[k])
                getattr(block, eng)(body)
        self.stack.close()
        self.cm.__exit__(None, None, None)


class Cfg:
    def __init__(self, D=4096, S=8192, DFF=11008):
        self.D, self.S, self.DFF = D, S, DFF
        self.KC = D // 128
        self.AW = D // 2
        self.CW = D // 2
        self.NH = self.AW // 256
        self.QK = self.NH * 256
        self.WIN = 2 * self.QK + self.AW + 2 * self.CW
        self.TOWN = S // 4
        self.TE = self.TOWN + 128
        self.NT = self.TE // 128
        self.NKT = S // 128
        self.tblocks = []
        t = 0
        while t < self.TE:
            rem = self.TE - t
            if rem == 640:
                w = 384
            else:
                w = min(512, rem)
            self.tblocks.append((t, w))
            t += w
        self.sblocks = [(t, 512) for t in range(0, S, 512)]
        self.QOFF = 192

    def near(self, qs, qw, j):
        dj = 128 * j - self.QOFF - qs
        return (-218 < dj < qw + 90), dj


def t5_bucket_np(rel):
    nb = N_BUCKETS // 2
    max_exact = nb // 2
    ret = np.where(rel > 0, nb, 0)
    n = np.abs(rel)
    nf = np.maximum(n, max_exact).astype(np.float32)
    large = max_exact + (np.log(nf / max_exact) / math.log(MAX_DISTANCE / max_exact)
                         * (nb - max_exact)).astype(np.int32)
    large = np.minimum(large, nb - 1)
    return ret + np.where(n < max_exact, n, large)


def kblocks_of(kc_total, kcb):
    out = []
    k = 0
    while k < kc_total:
        n = min(kcb, kc_total - k)
        out.append((k, n))
        k += n
    return out


class WScr:
    def __init__(self, nc, name, K, N, npw, kcb):
        self.K, self.N, self.npw = K, N, npw
        self.kbs = kblocks_of(K // 128, kcb)
        self.npan = N // npw
        self.t = {}
        for p in range(self.npan):
            for bi, (k0, n) in enumerate(self.kbs):
                self.t[(p, bi)] = nc.dram_tensor(f"ws_{name}_{p}_{bi}", [128, n * npw], BF16).ap()


def build_program(cfg, dbg=()):
    nc = bass.Bass("TRN2", target_bir_lowering=False)
    D, S, DFF, KC, NH, CW, TE, NT = cfg.D, cfg.S, cfg.DFF, cfg.KC, cfg.NH, cfg.CW, cfg.TE, cfg.NT
    CC = CW // 128
    FC = DFF // 128

    def din(name, shape, dt=F32):
        return nc.dram_tensor(name, list(shape), dt, kind="ExternalInput").ap()

    def scr(name, shape, dt):
        if name in dbg:
            return nc.dram_tensor(name, list(shape), dt, kind="ExternalOutput").ap()
        return nc.dram_tensor(name, list(shape), dt).ap()

    xown = din("xown", [TE, D])
    xseq = din("xseq", [S, D])
    maskd = din("mask", [128, NT])
    w_ev_in = din("ev_w_in", [D, cfg.WIN])
    w_ev_out = din("ev_w_out", [D, D])
    w_od_in = din("od_w_in", [D, 3 * D])
    w_od_out = din("od_w_out", [D, D])
    w_gate = din("ffn_w_gate", [2 * D, DFF])
    w_up = din("ffn_w_up", [2 * D, DFF])
    w_down = din("ffn_w_down", [2 * DFF, D])
    grep = din("grep", [8 * 128, D])
    conf_w = din("conf_w", [128, CC * CONF_K])
    conf_v = din("conf_v", [128, 3 * CC])
    od_cw = din("od_cw", [128, KC * 3])
    ffn_cw = din("ffn_cw", [128, 2 * FC * 3])
    ffn_cb = din("ffn_cb", [128, 2 * FC])
    subg = din("subg", [128, 256])
    lamv = din("lamv", [128, 4 * 128])
    wtoep_w = 512 + 768
    wtoep = din("wtoep", [NH * 128, wtoep_w])
    NQB = len(cfg.tblocks)
    cfar = din("cfar", [128, NQB * cfg.NKT * NH])
    nmd = din("nm", [128, cfg.NKT])
    fcnd = din("fcn", [128, cfg.NKT * NH])
    identd = din("ident", [128, 128])
    yout = nc.dram_tensor("y", [cfg.TOWN, D], F32, kind="ExternalOutput").ap()

    ws_ev_in = WScr(nc, "evin", D, cfg.WIN, 256, KC)
    ws_ev_out = WScr(nc, "evout", D, D, 512, 16)
    ws_od_in = WScr(nc, "odin", D, 3 * D, 256, KC)
    ws_od_out = WScr(nc, "odout", D, D, 512, 16)
    ws_gate = [WScr(nc, f"gate{l}", D, DFF, 256, KC) for l in range(2)]
    ws_up = [WScr(nc, f"up{l}", D, DFF, 256, KC) for l in range(2)]
    ws_down = [WScr(nc, f"down{l}", DFF, D, 512, 16) for l in range(2)]
    QT = scr("QT", [cfg.QK, TE], BF16)
    KT = scr("KT", [cfg.QK, S], BF16)
    VV = scr("VV", [S, cfg.AW], BF16)
    UU = scr("UU", [CW, TE], F32)
    UA = scr("UA", [CW, TE], BF16)
    AT = scr("AT", [cfg.AW, TE], BF16)
    MM = scr("MM", [TE, D], F32)
    SS = scr("SS", [128, NT * (D // 512)], F32)
    X1 = scr("X1", [TE, D], F32)
    X2 = scr("X2", [TE, D], F32)
    X3 = scr("X3", [TE, D], F32)
    GG = scr("GG", [DFF, TE], F32)
    UP = scr("UP", [DFF, TE], BF16)
    HT = scr("HT", [DFF, TE], BF16)
    GB = scr("GB", [D, TE], BF16)
    PP = scr("PP", [D, TE], F32)
    YT = scr("YT", [D, TE], BF16)

    alt = [0]
    phc = [0]

    def skip():
        phc[0] += 1
        return phc[0] > PH_LIMIT[0]

    def alt_eng():
        alt[0] ^= 1
        return "vector" if alt[0] else "scalar"

    def copy_op(ph, eng, out_ap, in_ap, reads, writes):
        if eng == "vector":
            ph.op("vector", I("tensor_copy", out=out_ap, in_=in_ap), reads=reads, writes=writes)
        else:
            ph.op("scalar", I("copy", out=out_ap, in_=in_ap), reads=reads, writes=writes)

    def conv_items(src, row0, ws, panels=None):
        items = []
        npw = ws.npw
        for p in (range(ws.npan) if panels is None else panels):
            for bi, (k0, n) in enumerate(ws.kbs):
                for c0 in range(0, n, 16):
                    nn = min(16, n - c0)
                    r0 = row0 + (k0 + c0) * 128
                    items.append((ws.t[(p, bi)][:, c0 * npw:(c0 + nn) * npw].rearrange("p (k n) -> p k n", n=npw),
                                  src[r0:r0 + nn * 128, p * npw:(p + 1) * npw].rearrange("(k p) n -> p k n", p=128)))
        return items

    def phase_bgonly(name, items):
        if skip():
            return
        ph = Phase(nc, name)
        ph.add_bg(items)
        ph.run()

    def phase_wconv(name, items):
        if skip():
            return
        ph = Phase(nc, name)
        NS = 3
        st32 = [ph.sbuf(f"s32_{i}", [128, 4096], F32) for i in range(NS)]
        st16 = [ph.sbuf(f"s16_{i}", [128, 4096], BF16) for i in range(NS)]
        u = 0
        for (src, row0, ws) in items:
            npw = ws.npw
            kcs_max = 4096 // npw
            for p in range(ws.npan):
                for bi, (k0, n) in enumerate(ws.kbs):
                    c = 0
                    while c < n:
                        kcs = min(kcs_max, n - c)
                        a, b = st32[u % NS], st16[u % NS]
                        r0 = row0 + (k0 + c) * 128
                        sap = src[r0:r0 + kcs * 128, p * npw:(p + 1) * npw].rearrange("(k p) n -> p k n", p=128)
                        ph.load(a.t[:, 0:kcs * npw].rearrange("p (k n) -> p k n", n=npw), sap, writes=[a])
                        copy_op(ph, alt_eng(), b.t[:, 0:kcs * npw], a.t[:, 0:kcs * npw], [a], [b])
                        ph.store(ws.t[(p, bi)][:, c * npw:(c + kcs) * npw], b.t[:, 0:kcs * npw], reads=[b])
                        c += kcs
                        u += 1
        ph.run()

    def phase_inproj(name, xsrc, blocks, g_idx, ws, jobs, bg=()):
        if skip():
            return
        ph = Phase(nc, name)
        ph.add_bg(bg)
        ident32 = ph.sbuf("id32", [128, 128], F32)
        identb = ph.sbuf("idb", [128, 128], BF16)
        grp = ph.sbuf("grp", [128, D], F32)
        epsb = ph.sbuf("epsb", [128, 1], F32)
        ph.load(ident32.t[:], identd[:, :], writes=[ident32])
        ph.op("vector", I("tensor_copy", out=identb.t[:], in_=ident32.t[:]), reads=[ident32], writes=[identb])
        ph.load(grp.t[:], grep[g_idx * 128:(g_idx + 1) * 128, :], writes=[grp])
        ph.op("vector", I("memset", ap=epsb.t[:], constant=EPS), writes=[epsb])
        xt = [ph.sbuf(f"xt{i}", [128, D], F32) for i in range(2)]
        xs = [ph.sbuf(f"xs{i}", [128, D], BF16) for i in range(2)]
        junk = ph.sbuf("junk", [128, D], BF16)
        ssq = [ph.sbuf(f"ssq{i}", [128, 1], F32) for i in range(2)]
        rst = [ph.sbuf(f"rst{i}", [128, 1], F32) for i in range(2)]
        hnT = ph.sbuf("hnT", [128, KC, 512], BF16)
        TB = min(8, KC)
        ptr = [ph.psum(f"ptr{i}", [128, 1024], BF16) for i in range(2)]
        pacc = [ph.psum(f"pacc{i}", [128, 512], F32) for i in range(4)]
        NWS = 3
        wsl = [ph.sbuf(f"w{i}", [128, KC * 256], BF16) for i in range(NWS)]
        stg = [ph.sbuf(f"stg{i}", [128, 512], F32) for i in range(4)]
        stgb = [ph.sbuf(f"stgb{i}", [128, 512], BF16) for i in range(4)]
        sgt = [ph.sbuf(f"sg{i}", [128, 512], F32) for i in range(2)]
        cnt = {"x": 0, "tr": 0, "w": 0, "acc": 0, "stg": 0, "sg": 0, "gb": 0}
        has_ffn = any(jb["kind"] == "ffn_up" for jb in jobs)
        if has_ffn:
            fcw = ph.sbuf("fcw", [128, 2 * FC * 3], F32)
            fcb = ph.sbuf("fcb", [128, 2 * FC], F32)
            ph.load(fcw.t[:], ffn_cw[:, :], writes=[fcw])
            ph.load(fcb.t[:], ffn_cb[:, :], writes=[fcb])
            gbuf = [ph.sbuf(f"gbuf{i}", [128, 514], F32) for i in range(3)]
            abuf = [ph.sbuf(f"abuf{i}", [128, 512], F32) for i in range(3)]

        def wload(pi):
            sl = wsl[cnt["w"] % NWS]
            cnt["w"] += 1
            ph.load(sl.t[:], ws.t[(pi, 0)][:, :], writes=[sl])
            return sl

        def accgroup(ps, sl, j, w):
            fn = [I("matmul", out=ps.t[:, 0:w], lhsT=sl.t[:, kc * 256 + j * 128: kc * 256 + (j + 1) * 128],
                    rhs=hnT.t[:, kc, 0:w], start=(kc == 0), stop=(kc == KC - 1)) for kc in range(KC)]
            ph.op("tensor", fn, reads=[sl, hnT], writes=[ps])

        for (tok0, w) in blocks:
            nt = w // 128
            for t in range(nt):
                i = cnt["x"] % 2
                cnt["x"] += 1
                a, b, s1, r1 = xt[i], xs[i], ssq[i], rst[i]
                ph.load(a.t[:], xsrc[tok0 + t * 128: tok0 + (t + 1) * 128, :], writes=[a])
                ph.op("scalar", I("activation", out=junk.t[:], in_=a.t[:], func=AF.Square, accum_out=s1.t[:]),
                      reads=[a], writes=[junk, s1])
                ph.op("scalar", I("activation", out=r1.t[:], in_=s1.t[:], func=AF.Sqrt, scale=1.0 / D, bias=epsb.t[:, 0:1]),
                      reads=[s1, epsb], writes=[r1])
                ph.op("vector", I("reciprocal", out=r1.t[:], in_=r1.t[:]), reads=[r1], writes=[r1])
                ph.op("vector", I("scalar_tensor_tensor", out=b.t[:], in0=a.t[:], scalar=r1.t[:, 0:1], in1=grp.t[:],
                                                                                 op0=ALU.mult, op1=ALU.mult),
                      reads=[a, r1, grp], writes=[b])
                for c0 in range(0, KC, TB):
                    pt = ptr[cnt["tr"] % 2]
                    cnt["tr"] += 1

                    ftr = [I("transpose", out=pt.t[:, c * 128:(c + 1) * 128], in_=b.t[:, (c0 + c) * 128:(c0 + c + 1) * 128],
                             identity=identb.t[:]) for c in range(TB)]
                    ph.op("tensor", ftr, reads=[b, identb], writes=[pt])
                    copy_op(ph, alt_eng(), hnT.t[:, c0:c0 + TB, t * 128:(t + 1) * 128],
                            pt.t[:, 0:TB * 128].rearrange("p (c k) -> p c k", k=128), [pt], [hnT])
            for job in jobs:
                kind = job["kind"]
                if kind == "single":
                    for pi in job["panels"]:
                        sl = wload(pi)
                        for j in range(2):
                            ps = pacc[cnt["acc"] % 4]
                            cnt["acc"] += 1
                            accgroup(ps, sl, j, w)
                            row = (pi - job["panels"][0]) * 256 + j * 128
                            k = cnt["stg"] % 4
                            cnt["stg"] += 1
                            if job["dt"] == BF16:
                                sb = stgb[k]
                            else:
                                sb = stg[k]
                            copy_op(ph, alt_eng(), sb.t[:, 0:w], ps.t[:, 0:w], [ps], [sb])
                            ph.store(job["out"][row:row + 128, tok0:tok0 + w], sb.t[:, 0:w], reads=[sb])
                elif kind in ("glu", "mul"):
                    for pa, pb in zip(job["panels_a"], job["panels_b"]):
                        sla = wload(pa)
                        slb = wload(pb)
                        for j in range(2):
                            psa = pacc[cnt["acc"] % 4]
                            psb = pacc[(cnt["acc"] + 1) % 4]
                            cnt["acc"] += 2
                            accgroup(psa, sla, j, w)
                            accgroup(psb, slb, j, w)
                            row = (pa - job["panels_a"][0]) * 256 + j * 128
                            sg = sgt[cnt["sg"] % 2]
                            cnt["sg"] += 1
                            k = cnt["stg"] % 4
                            cnt["stg"] += 1
                            sb = stg[k]
                            fnc = AF.Sigmoid if kind == "glu" else AF.Copy
                            ph.op("scalar", I("activation", out=sg.t[:, 0:w], in_=psb.t[:, 0:w], func=fnc),
                                  reads=[psb], writes=[sg])
                            ph.op("vector", I("tensor_tensor", out=sb.t[:, 0:w], in0=psa.t[:, 0:w], in1=sg.t[:, 0:w], op=ALU.mult),
                                  reads=[psa, sg], writes=[sb])
                            ph.store(job["out"][row:row + 128, tok0:tok0 + w], sb.t[:, 0:w], reads=[sb])
                elif kind == "ffn_up":
                    layer = job["layer"]
                    for pi in job["panels"]:
                        sl = wload(pi)
                        for j in range(2):
                            c = pi * 2 + j
                            gbf = gbuf[cnt["gb"] % 3]
                            ab = abuf[cnt["gb"] % 3]
                            cnt["gb"] += 1
                            lo = max(tok0 - 1, 0)
                            hi = min(tok0 + w + 1, TE)
                            if lo > tok0 - 1 or hi < tok0 + w + 1:
                                ph.op("vector", I("memset", ap=gbf.t[:], constant=0.0), writes=[gbf])
                            ph.load(gbf.t[:, lo - (tok0 - 1):hi - (tok0 - 1)], GG[c * 128:(c + 1) * 128, lo:hi], writes=[gbf])
                            wb = (layer * FC + c) * 3
                            bb = layer * FC + c
                            fconv = [I("tensor_scalar", out=ab.t[:, 0:w], in0=gbf.t[:, 0:w], scalar1=fcw.t[:, wb:wb + 1], scalar2=fcb.t[:, bb:bb + 1], op0=ALU.mult, op1=ALU.add),
                                     I("scalar_tensor_tensor", out=ab.t[:, 0:w], in0=gbf.t[:, 1:w + 1], scalar=fcw.t[:, wb + 1:wb + 2], in1=ab.t[:, 0:w], op0=ALU.mult, op1=ALU.add),
                                     I("scalar_tensor_tensor", out=ab.t[:, 0:w], in0=gbf.t[:, 2:w + 2], scalar=fcw.t[:, wb + 2:wb + 3], in1=ab.t[:, 0:w], op0=ALU.mult, op1=ALU.add)]
                            ph.op("vector", fconv, reads=[gbf, fcw, fcb], writes=[ab])
                            ph.op("scalar", I("activation", out=ab.t[:, 0:w], in_=ab.t[:, 0:w], func=AF.Gelu_apprx_tanh), reads=[ab], writes=[ab])
                            ps = pacc[cnt["acc"] % 4]
                            cnt["acc"] += 1
                            accgroup(ps, sl, j, w)
                            k = cnt["stg"] % 4
                            cnt["stg"] += 1
                            sb = stgb[k]
                            ph.op("vector", I("tensor_tensor", out=sb.t[:, 0:w], in0=ps.t[:, 0:w], in1=ab.t[:, 0:w], op=ALU.mult), reads=[ps, ab], writes=[sb])
                            ph.store(job["out"][c * 128:(c + 1) * 128, tok0:tok0 + w], sb.t[:, 0:w], reads=[sb])
                elif kind == "tok":
                    for pi in job["panels"]:
                        sl = wload(pi)
                        for t in range(nt):
                            ps = pacc[cnt["acc"] % 4]
                            cnt["acc"] += 1

                            fn = [I("matmul", out=ps.t[:, 0:256], lhsT=hnT.t[:, kc, t * 128:(t + 1) * 128],
                                    rhs=sl.t[:, kc * 256:(kc + 1) * 256], start=(kc == 0), stop=(kc == KC - 1)) for kc in range(KC)]
                            ph.op("tensor", fn, reads=[sl, hnT], writes=[ps])
                            k = cnt["stg"] % 4
                            cnt["stg"] += 1
                            sb = stgb[k]
                            copy_op(ph, alt_eng(), sb.t[:, 0:256], ps.t[:, 0:256], [ps], [sb])
                            col = (pi - job["panels"][0]) * 256
                            ph.store(job["out"][tok0 + t * 128: tok0 + (t + 1) * 128, col:col + 256], sb.t[:, 0:256], reads=[sb])
        ph.run()

    def phase_outproj(name, srcs, ws):
        if skip():
            return
        ph = Phase(nc, name)
        kct = sum(k for _, k in srcs)
        NP = D // 512
        NWS = 3
        L = ph.sbuf("L", [128, kct, 512], BF16)
        wsl = [ph.sbuf(f"w{i}", [128, 16 * 512], BF16) for i in range(NWS)]
        pacc = [ph.psum(f"pacc{i}", [128, 512], F32) for i in range(8)]
        stg = [ph.sbuf(f"stg{i}", [128, 512], F32) for i in range(4)]
        junk = ph.sbuf("junk", [128, 512], BF16)
        ssp = ph.sbuf("ssp", [128, NT * NP], F32)
        ph.op("vector", I("memset", ap=ssp.t[:], constant=0.0), writes=[ssp])
        cnt = {"w": 0, "stg": 0, "g": 0}
        for (tok0, w) in cfg.tblocks:
            nt = w // 128
            kk = 0
            for (src, kcs) in srcs:
                for c0 in range(0, kcs, 16):
                    n = min(16, kcs - c0)
                    ph.load(L.t[:, kk + c0: kk + c0 + n, 0:w],
                            src[c0 * 128:(c0 + n) * 128, tok0:tok0 + w].rearrange("(k p) t -> p k t", p=128), writes=[L])
                kk += kcs
            for pn in range(NP):
                accs = pacc[(cnt["g"] % 2) * 4:(cnt["g"] % 2) * 4 + 4]
                cnt["g"] += 1
                nkb = len(ws.kbs)
                for bi, (k0, n) in enumerate(ws.kbs):
                    sl = wsl[cnt["w"] % NWS]
                    cnt["w"] += 1
                    ph.load(sl.t[:, 0:n * 512], ws.t[(pn, bi)][:, :], writes=[sl])
                    for t in range(nt):
                        fn = [I("matmul", out=accs[t].t[:, :], lhsT=L.t[:, k0 + c, t * 128:(t + 1) * 128], rhs=sl.t[:, c * 512:(c + 1) * 512],
                                start=(bi == 0 and c == 0), stop=(bi == nkb - 1 and c == n - 1)) for c in range(n)]
                        ph.op("tensor", fn, reads=[sl, L], writes=[accs[t]])
                for t in range(nt):
                    tile_i = tok0 // 128 + t
                    sb = stg[cnt["stg"] % 4]
                    cnt["stg"] += 1
                    acc = accs[t]
                    ph.op("vector", I("tensor_copy", out=sb.t[:], in_=acc.t[:]), reads=[acc], writes=[sb])
                    col = tile_i * NP + pn
                    ph.op("scalar", I("activation", out=junk.t[:], in_=acc.t[:], func=AF.Square, accum_out=ssp.t[:, col:col + 1]),
                          reads=[], writes=[junk, ssp, acc])
                    ph.store(MM[tile_i * 128:(tile_i + 1) * 128, pn * 512:(pn + 1) * 512], sb.t[:], reads=[sb])
        ph.store(SS[:, :], ssp.t[:], reads=[ssp])
        ph.run()

    def phase_resid(name, g_idx, xin, xout, final=False):
        if skip():
            return
        ph = Phase(nc, name)
        NP = D // 512
        grp = ph.sbuf("grp", [128, D], F32)
        ssp = ph.sbuf("ssp", [128, NT * NP], F32)
        msk = ph.sbuf("msk", [128, NT], F32)
        epsb = ph.sbuf("epsb", [128, 1], F32)
        ph.load(grp.t[:], grep[g_idx * 128:(g_idx + 1) * 128, :], writes=[grp])
        ph.load(ssp.t[:], SS[:, :], writes=[ssp])
        ph.load(msk.t[:], maskd[:, :], writes=[msk])
        ph.op("vector", I("memset", ap=epsb.t[:], constant=EPS), writes=[epsb])
        mt = [ph.sbuf(f"m{i}", [128, D], F32) for i in range(2)]
        xt = [ph.sbuf(f"x{i}", [128, D], F32) for i in range(2)]
        ss1 = [ph.sbuf(f"ss{i}", [128, 1], F32) for i in range(2)]
        tiles = range(NT)
        for n, ti in enumerate(tiles):
            if final and (ti * 128 + 128 <= 64 or ti * 128 >= 64 + cfg.TOWN):
                pass
            m, x, s1 = mt[n % 2], xt[n % 2], ss1[n % 2]
            ph.load(m.t[:], MM[ti * 128:(ti + 1) * 128, :], writes=[m])
            ph.load(x.t[:], xin[ti * 128:(ti + 1) * 128, :], writes=[x])
            ph.op("vector", I("reduce_sum", out=s1.t[:], in_=ssp.t[:, ti * NP:(ti + 1) * NP], axis=mybir.AxisListType.X),
                  reads=[ssp], writes=[s1])
            ph.op("scalar", I("activation", out=s1.t[:], in_=s1.t[:], func=AF.Sqrt, scale=1.0 / D, bias=epsb.t[:, 0:1]),
                  reads=[s1, epsb], writes=[s1])
            ph.op("vector", I("reciprocal", out=s1.t[:], in_=s1.t[:]), reads=[s1], writes=[s1])
            ph.op("vector", I("tensor_tensor", out=s1.t[:], in0=s1.t[:], in1=msk.t[:, ti:ti + 1], op=ALU.mult),
                  reads=[s1, msk], writes=[s1])
            ph.op("vector", I("tensor_tensor", out=m.t[:], in0=m.t[:], in1=grp.t[:], op=ALU.mult), reads=[m, grp], writes=[m])
            ph.op("vector", I("scalar_tensor_tensor", out=x.t[:], in0=m.t[:], scalar=s1.t[:, 0:1], in1=x.t[:],
                                                                             op0=ALU.mult, op1=ALU.add),
                  reads=[m, s1, x], writes=[x])
            if not final:
                ph.store(xout[ti * 128:(ti + 1) * 128, :], x.t[:], reads=[x])
            else:
                lo = max(ti * 128, 64)
                hi = min(ti * 128 + 128, 64 + cfg.TOWN)
                if hi > lo:
                    ph.store(xout[lo - 64:hi - 64, :], x.t[lo - ti * 128:hi - ti * 128, :], reads=[x])
        ph.run()

    def phase_conformer(name, bg=()):
        if skip():
            return
        ph = Phase(nc, name)
        ph.add_bg(bg)
        PAD = (CONF_K - 1) // 2
        cw = ph.sbuf("cw", [128, CC * CONF_K], F32)
        cv = ph.sbuf("cv", [128, 3 * CC], F32)
        ones = ph.sbuf("ones", [128, 128], F32)
        ph.load(cw.t[:], conf_w[:, :], writes=[cw])
        ph.load(cv.t[:], conf_v[:, :], writes=[cv])
        ph.op("vector", I("memset", ap=ones.t[:], constant=1.0), writes=[ones])
        uin = [ph.sbuf(f"uin{i}", [128, 512 + 2 * PAD], F32) for i in range(3)]
        cout = [ph.sbuf(f"co{i}", [128, 512], F32) for i in range(CC)]
        sq = [ph.sbuf(f"sq{i}", [128, 512], F32) for i in range(2)]
        psum_s = ph.psum("ps_s", [128, 512], F32)
        psum_q = ph.psum("ps_q", [128, 512], F32)
        mean = ph.sbuf("mean", [128, 512], F32)
        rstd = ph.sbuf("rstd", [128, 512], F32)
        epsb = ph.sbuf("epsb", [128, 1], F32)
        ph.op("vector", I("memset", ap=epsb.t[:], constant=EPS), writes=[epsb])
        ob = [ph.sbuf(f"ob{i}", [128, 512], BF16) for i in range(3)]
        n_u = 0
        for (tok0, w) in cfg.tblocks:
            lo = max(tok0 - PAD, 0)
            hi = min(tok0 + w + PAD, TE)
            for c in range(CC):
                ui = uin[n_u % 3]
                n_u += 1
                if lo > tok0 - PAD or hi < tok0 + w + PAD:
                    ph.op("vector", I("memset", ap=ui.t[:], constant=0.0), writes=[ui])
                ph.load(ui.t[:, lo - (tok0 - PAD): hi - (tok0 - PAD)], UU[c * 128:(c + 1) * 128, lo:hi], writes=[ui])
                co = cout[c]

                fconv = [I("tensor_scalar", out=co.t[:, 0:w], in0=ui.t[:, 0:w], scalar1=cw.t[:, c * CONF_K:c * CONF_K + 1],
                           scalar2=cv.t[:, c:c + 1], op0=ALU.mult, op1=ALU.add)]
                for j in range(1, CONF_K):
                    fconv.append(I("scalar_tensor_tensor", out=co.t[:, 0:w], in0=ui.t[:, j:j + w], scalar=cw.t[:, c * CONF_K + j:c * CONF_K + j + 1],
                                   in1=co.t[:, 0:w], op0=ALU.mult, op1=ALU.add))
                ph.op("vector", fconv, reads=[ui, cw, cv], writes=[co])
                s = sq[c % 2]
                ph.op("scalar", I("activation", out=s.t[:, 0:w], in_=co.t[:, 0:w], func=AF.Square), reads=[co], writes=[s])
                ph.op("tensor", I("matmul", out=psum_s.t[:, 0:w], lhsT=ones.t[:], rhs=co.t[:, 0:w], start=(c == 0), stop=(c == CC - 1)),
                      reads=[ones, co], writes=[psum_s])
                ph.op("tensor", I("matmul", out=psum_q.t[:, 0:w], lhsT=ones.t[:], rhs=s.t[:, 0:w], start=(c == 0), stop=(c == CC - 1)),
                      reads=[ones, s], writes=[psum_q])
            ph.op("scalar", I("activation", out=mean.t[:, 0:w], in_=psum_s.t[:, 0:w], func=AF.Copy, scale=1.0 / CW), reads=[psum_s], writes=[mean])
            s = sq[0]
            ph.op("vector", I("tensor_tensor", out=s.t[:, 0:w], in0=mean.t[:, 0:w], in1=mean.t[:, 0:w], op=ALU.mult), reads=[mean], writes=[s])
            ph.op("vector", I("scalar_tensor_tensor", out=rstd.t[:, 0:w], in0=psum_q.t[:, 0:w], scalar=1.0 / CW, in1=s.t[:, 0:w],
                                                                op0=ALU.mult, op1=ALU.subtract), reads=[psum_q, s], writes=[rstd])
            ph.op("scalar", I("activation", out=rstd.t[:, 0:w], in_=rstd.t[:, 0:w], func=AF.Sqrt, bias=epsb.t[:, 0:1]), reads=[rstd, epsb], writes=[rstd])
            ph.op("vector", I("reciprocal", out=rstd.t[:, 0:w], in_=rstd.t[:, 0:w]), reads=[rstd], writes=[rstd])
            for c in range(CC):
                co = cout[c]
                o = ob[c % 3]
                ph.op("vector", I("tensor_tensor", out=co.t[:, 0:w], in0=co.t[:, 0:w], in1=mean.t[:, 0:w], op=ALU.subtract), reads=[co, mean], writes=[co])
                ph.op("vector", I("tensor_tensor", out=co.t[:, 0:w], in0=co.t[:, 0:w], in1=rstd.t[:, 0:w], op=ALU.mult), reads=[co, rstd], writes=[co])
                ph.op("scalar", I("activation", out=o.t[:, 0:w], in_=co.t[:, 0:w], func=AF.Silu,
                                                                       scale=cv.t[:, CC + c:CC + c + 1], bias=cv.t[:, 2 * CC + c:2 * CC + c + 1]),
                      reads=[co, cv], writes=[o])
                ph.store(UA[c * 128:(c + 1) * 128, tok0:tok0 + w], o.t[:, 0:w], reads=[o])
        ph.run()

    def phase_attention(name, lam_init, bg=()):
        if skip():
            return
        ph = Phase(nc, name)
        ph.add_bg(bg)
        NKT = cfg.NKT
        scale = 128 ** -0.5
        ident32 = ph.sbuf("id32", [128, 128], F32)
        identb = ph.sbuf("idb", [128, 128], BF16)
        ph.load(ident32.t[:], identd[:, :], writes=[ident32])
        ph.op("vector", I("tensor_copy", out=identb.t[:], in_=ident32.t[:]), reads=[ident32], writes=[identb])
        epsb = ph.sbuf("epsb", [128, 1], F32)
        ph.op("vector", I("memset", ap=epsb.t[:], constant=EPS), writes=[epsb])
        lv = ph.sbuf("lv", [128, 512], F32)
        ph.load(lv.t[:], lamv[:, :], writes=[lv])
        lt = ph.sbuf("lt", [128, 256], F32)
        l2 = ph.sbuf("l2", [128, 2], F32)
        neglam = ph.sbuf("neglam", [128, 1], F32)
        ph.op("vector", I("tensor_tensor", out=lt.t[:, 0:128], in0=lv.t[:, 0:128], in1=lv.t[:, 128:256], op=ALU.mult), reads=[lv], writes=[lt])
        ph.op("vector", I("tensor_tensor", out=lt.t[:, 128:256], in0=lv.t[:, 256:384], in1=lv.t[:, 384:512], op=ALU.mult), reads=[lv, lt], writes=[lt])
        ph.op("vector", I("reduce_sum", out=l2.t[:, 0:1], in_=lt.t[:, 0:128], axis=mybir.AxisListType.X), reads=[lt], writes=[l2])
        ph.op("vector", I("reduce_sum", out=l2.t[:, 1:2], in_=lt.t[:, 128:256], axis=mybir.AxisListType.X), reads=[lt, l2], writes=[l2])
        ph.op("scalar", I("activation", out=l2.t[:], in_=l2.t[:], func=AF.Exp), reads=[l2], writes=[l2])
        ph.op("vector", I("scalar_tensor_tensor", out=neglam.t[:], in0=l2.t[:, 1:2], scalar=-lam_init, in1=l2.t[:, 0:1],
                                                          op0=ALU.add, op1=ALU.subtract), reads=[l2], writes=[neglam])
        sgp = ph.sbuf("sgp", [128, 256], F32)
        ph.load(sgp.t[:], subg[:, :], writes=[sgp])
        cft = ph.sbuf("cft", [128, NQB * NKT * NH], F32)
        nmt = ph.sbuf("nmt", [128, NKT], F32)
        fct = ph.sbuf("fct", [128, NKT * NH], F32)
        ph.load(cft.t[:], cfar[:, :], writes=[cft])
        ph.load(nmt.t[:], nmd[:, :], writes=[nmt])
        ph.load(fct.t[:], fcnd[:, :], writes=[fct])
        ktb2 = [ph.sbuf(f"ktb{i}", [128, 2, S], BF16) for i in range(2)]
        vtb2 = [ph.sbuf(f"vtb{i}", [128, NKT, 257], BF16) for i in range(2)]
        qtb2 = [ph.sbuf(f"qtb{i}", [128, 2, TE], BF16) for i in range(2)]
        wtp2 = [ph.sbuf(f"wtp{i}", [128, wtoep_w], F32) for i in range(2)]

        def head_loads(h):
            ktb, vtb, qtb, wtp = ktb2[h % 2], vtb2[h % 2], qtb2[h % 2], wtp2[h % 2]
            for m in range(2):
                ph.load(ktb.t[:, m, :], KT[(h * 2 + m) * 128:(h * 2 + m + 1) * 128, :], writes=[ktb])
                ph.load(qtb.t[:, m, :], QT[(h * 2 + m) * 128:(h * 2 + m + 1) * 128, :], writes=[qtb])
            for k0 in range(0, NKT, 16):
                n = min(16, NKT - k0)
                ph.load(vtb.t[:, k0:k0 + n, 0:256],
                        VV[k0 * 128:(k0 + n) * 128, h * 256:(h + 1) * 256].rearrange("(k p) e -> p k e", p=128), writes=[vtb])
            ph.op("vector", I("memset", ap=vtb.t[:, :, 256:257], constant=1.0), writes=[vtb])
            ph.load(wtp.t[:], wtoep[h * 128:(h + 1) * 128, :], writes=[wtp])
        psS = [ph.psum(f"psS{i}", [128, 512], F32) for i in range(3)]
        pacc = [ph.psum(f"pacc{i}", [128, 512], F32) for i in range(4)]
        ptr = ph.psum("ptr", [128, 256], BF16)
        pt = [ph.sbuf(f"pt{i}", [128, 512], BF16) for i in range(4)]
        bt = [ph.sbuf(f"bt{i}", [128, 512], F32) for i in range(2)]
        tmp = [ph.sbuf(f"tmp{i}", [128, 512], F32) for i in range(2)]
        om = [[ph.sbuf(f"om{m}_{s}", [128, 257], F32) for s in range(4)] for m in range(2)]
        rr = [ph.sbuf(f"rr{i}", [128, 4], F32) for i in range(2)]
        oc = [ph.sbuf(f"oc{i}", [128, 256], F32) for i in range(2)]
        ocb = [ph.sbuf(f"ocb{i}", [128, 256], BF16) for i in range(2)]
        junk = ph.sbuf("junk", [128, 256], BF16)
        ath = [ph.sbuf("ath0", [128, 2, TE], BF16)] * 2
        cnt = {"s": 0, "p": 0, "b": 0, "o": 0}
        gain = 1.0 - lam_init
        head_loads(0)
        for h in range(NH):
            if h + 1 < NH:
                head_loads(h + 1)
            ktb, vtb, qtb, wtp = ktb2[h % 2], vtb2[h % 2], qtb2[h % 2], wtp2[h % 2]
            units = [(qb, qs, qw, m, j) for qb, (qs, qw) in enumerate(cfg.tblocks) for m in range(2) for j in range(NKT)]
            pbuf = {}
            LA = 2

            def emit_qk(u):
                qb, qs, qw, m, j = units[u]
                ps = psS[u % 3]
                ph.op("tensor", I("matmul", out=ps.t[:, 0:qw], lhsT=ktb.t[:, m, j * 128:(j + 1) * 128], rhs=qtb.t[:, m, qs:qs + qw],
                                  start=True, stop=True), reads=[ktb, qtb], writes=[ps])
                p = pt[u % 4]
                pbuf[u] = p
                isnear, dj = cfg.near(qs, qw, j)
                if isnear:
                    c0 = 576 - dj
                    b = bt[cnt["b"] % 2]
                    tm = tmp[cnt["b"] % 2]
                    cnt["b"] += 1
                    ph.op("vector", I("tensor_scalar", out=b.t[:, 0:qw], in0=wtp.t[:, c0:c0 + qw], scalar1=nmt.t[:, j:j + 1],
                                      scalar2=fct.t[:, j * NH + h:j * NH + h + 1], op0=ALU.mult, op1=ALU.add),
                          reads=[wtp, nmt, fct], writes=[b])
                    ph.op("vector", I("scalar_tensor_tensor", out=tm.t[:, 0:qw], in0=ps.t[:, 0:qw], scalar=scale, in1=b.t[:, 0:qw],
                                      op0=ALU.mult, op1=ALU.add), reads=[ps, b], writes=[tm])
                    ph.op("scalar", I("activation", out=p.t[:, 0:qw], in_=tm.t[:, 0:qw], func=AF.Exp), reads=[tm], writes=[p])
                else:
                    ci = (qb * NKT + j) * NH + h
                    ph.op("scalar", I("activation", out=p.t[:, 0:qw], in_=ps.t[:, 0:qw], func=AF.Exp, scale=scale,
                                      bias=cft.t[:, ci:ci + 1]), reads=[ps, cft], writes=[p])

            def emit_pv(u):
                qb, qs, qw, m, j = units[u]
                nsub = qw // 128
                p = pbuf.pop(u)
                fpv = [I("matmul", out=pacc[s].t[:, 0:257], lhsT=p.t[:, s * 128:(s + 1) * 128], rhs=vtb.t[:, j, :],
                         start=(j == 0), stop=(j == NKT - 1)) for s in range(nsub)]
                ph.op("tensor", fpv, reads=[p, vtb], writes=pacc[0:nsub])
                if j != NKT - 1:
                    return
                for s in range(nsub):
                    o = om[m][s]
                    ph.op("vector", I("tensor_copy", out=o.t[:], in_=pacc[s].t[:, 0:257]), reads=[pacc[s]], writes=[o])
                if m != 1:
                    return
                for s in range(nsub):
                    i = cnt["o"] % 2
                    cnt["o"] += 1
                    r, o, obf, ab = rr[i], oc[i], ocb[i], ath[h % 2]
                    o0, o1 = om[0][s], om[1][s]
                    ph.op("vector", I("reciprocal", out=r.t[:, 0:1], in_=o0.t[:, 256:257]), reads=[o0], writes=[r])
                    ph.op("vector", I("reciprocal", out=r.t[:, 1:2], in_=o1.t[:, 256:257]), reads=[o1, r], writes=[r])
                    ph.op("vector", I("tensor_tensor", out=r.t[:, 1:2], in0=r.t[:, 1:2], in1=neglam.t[:, 0:1], op=ALU.mult), reads=[r, neglam], writes=[r])
                    ph.op("vector", I("tensor_scalar", out=o1.t[:, 0:256], in0=o1.t[:, 0:256], scalar1=r.t[:, 1:2], scalar2=None, op0=ALU.mult),
                          reads=[o1, r], writes=[o1])
                    ph.op("vector", I("scalar_tensor_tensor", out=o.t[:], in0=o0.t[:, 0:256], scalar=r.t[:, 0:1], in1=o1.t[:, 0:256],
                                      op0=ALU.mult, op1=ALU.add), reads=[o0, o1, r], writes=[o])
                    ph.op("scalar", I("activation", out=junk.t[:], in_=o.t[:], func=AF.Square, accum_out=r.t[:, 2:3]), reads=[o, r], writes=[junk, r])
                    ph.op("scalar", I("activation", out=r.t[:, 2:3], in_=r.t[:, 2:3], func=AF.Sqrt, scale=1.0 / 256, bias=epsb.t[:, 0:1]), reads=[r, epsb], writes=[r])
                    ph.op("vector", I("reciprocal", out=r.t[:, 3:4], in_=r.t[:, 2:3]), reads=[r], writes=[r])
                    ph.op("vector", I("tensor_scalar", out=r.t[:, 3:4], in0=r.t[:, 3:4], scalar1=gain, scalar2=None, op0=ALU.mult), reads=[r], writes=[r])
                    ph.op("vector", I("scalar_tensor_tensor", out=obf.t[:], in0=o.t[:], scalar=r.t[:, 3:4], in1=sgp.t[:],
                                      op0=ALU.mult, op1=ALU.mult), reads=[o, r, sgp], writes=[obf])
                    ftr = [I("transpose", out=ptr.t[:, 0:128], in_=obf.t[:, 0:128], identity=identb.t[:]),
                           I("transpose", out=ptr.t[:, 128:256], in_=obf.t[:, 128:256], identity=identb.t[:])]
                    ph.op("tensor", ftr, reads=[obf, identb], writes=[ptr])
                    t0 = qs + s * 128
                    copy_op(ph, "scalar", ab.t[:, :, t0:t0 + 128], ptr.t[:].rearrange("p (a k) -> p a k", k=128), [ptr], [ab])

            for idx in range(len(units) + LA):
                if idx < len(units):
                    emit_qk(idx)
                if idx - LA >= 0:
                    emit_pv(idx - LA)
            if ATT_CUT[0] >= 6:
                ph.store(AT[h * 256:h * 256 + 128, :], ath[h % 2].t[:, 0, :], reads=[ath[h % 2]])
                ph.store(AT[h * 256 + 128:h * 256 + 256, :], ath[h % 2].t[:, 1, :], reads=[ath[h % 2]])
        ph.run()

    def phase_ffn_elt(name, layer):
        if skip():
            return
        ph = Phase(nc, name)
        cw = ph.sbuf("cw", [128, 2 * FC * 3], F32)
        cb = ph.sbuf("cb", [128, 2 * FC], F32)
        ph.load(cw.t[:], ffn_cw[:, :], writes=[cw])
        ph.load(cb.t[:], ffn_cb[:, :], writes=[cb])
        gin = [ph.sbuf(f"gin{i}", [128, TE + 2], F32) for i in range(2)]
        upb = [ph.sbuf(f"up{i}", [128, TE], BF16) for i in range(2)]
        acc = [ph.sbuf(f"acc{i}", [128, TE], F32) for i in range(2)]
        hb = [ph.sbuf(f"hb{i}", [128, TE], BF16) for i in range(2)]
        for i in range(2):
            ph.op("vector", I("memset", ap=gin[i].t[:], constant=0.0), writes=[gin[i]])
        for c in range(FC):
            g, u, a, hh = gin[c % 2], upb[c % 2], acc[c % 2], hb[c % 2]
            ph.load(g.t[:, 1:TE + 1], GG[c * 128:(c + 1) * 128, :], writes=[g])
            ph.load(u.t[:], UP[c * 128:(c + 1) * 128, :], writes=[u])
            wb = (layer * FC + c) * 3
            bb = layer * FC + c

            fconv = [I("tensor_scalar", out=a.t[:], in0=g.t[:, 0:TE], scalar1=cw.t[:, wb:wb + 1], scalar2=cb.t[:, bb:bb + 1], op0=ALU.mult, op1=ALU.add),
                     I("scalar_tensor_tensor", out=a.t[:], in0=g.t[:, 1:TE + 1], scalar=cw.t[:, wb + 1:wb + 2], in1=a.t[:], op0=ALU.mult, op1=ALU.add),
                     I("scalar_tensor_tensor", out=a.t[:], in0=g.t[:, 2:TE + 2], scalar=cw.t[:, wb + 2:wb + 3], in1=a.t[:], op0=ALU.mult, op1=ALU.add)]
            ph.op("vector", fconv, reads=[g, cw, cb], writes=[a])
            ph.op("scalar", I("activation", out=a.t[:], in_=a.t[:], func=AF.Gelu_apprx_tanh), reads=[a], writes=[a])
            ph.op("vector", I("tensor_tensor", out=hh.t[:], in0=a.t[:], in1=u.t[:], op=ALU.mult), reads=[a, u], writes=[hh])
            ph.store(HT[c * 128:(c + 1) * 128, :], hh.t[:], reads=[hh])
        ph.run()

    def phase_odd_elt(name):
        if skip():
            return
        ph = Phase(nc, name)
        cw = ph.sbuf("cw", [128, KC * 3], F32)
        ph.load(cw.t[:], od_cw[:, :], writes=[cw])
        pin = [ph.sbuf(f"pin{i}", [128, TE + 2], F32) for i in range(2)]
        gb = [ph.sbuf(f"gb{i}", [128, TE], BF16) for i in range(2)]
        acc = [ph.sbuf(f"acc{i}", [128, TE], F32) for i in range(2)]
        yb = [ph.sbuf(f"yb{i}", [128, TE], BF16) for i in range(2)]
        for i in range(2):
            ph.op("vector", I("memset", ap=pin[i].t[:], constant=0.0), writes=[pin[i]])
        for c in range(KC):
            g, u, a, y = pin[c % 2], gb[c % 2], acc[c % 2], yb[c % 2]
            ph.load(g.t[:, 1:TE + 1], PP[c * 128:(c + 1) * 128, :], writes=[g])
            ph.load(u.t[:], GB[c * 128:(c + 1) * 128, :], writes=[u])
            wb = c * 3

            fconv = [I("tensor_scalar", out=a.t[:], in0=g.t[:, 0:TE], scalar1=cw.t[:, wb:wb + 1], scalar2=None, op0=ALU.mult),
                     I("scalar_tensor_tensor", out=a.t[:], in0=g.t[:, 1:TE + 1], scalar=cw.t[:, wb + 1:wb + 2], in1=a.t[:], op0=ALU.mult, op1=ALU.add),
                     I("scalar_tensor_tensor", out=a.t[:], in0=g.t[:, 2:TE + 2], scalar=cw.t[:, wb + 2:wb + 3], in1=a.t[:], op0=ALU.mult, op1=ALU.add)]
            ph.op("vector", fconv, reads=[g, cw], writes=[a])
            ph.op("vector", I("tensor_tensor", out=y.t[:], in0=a.t[:], in1=u.t[:], op=ALU.mult), reads=[a, u], writes=[y])
            ph.store(YT[c * 128:(c + 1) * 128, :], y.t[:], reads=[y])
        ph.run()

    qkp = cfg.QK // 256
    avp = cfg.AW // 256
    cwp = CW // 256
    fp = DFF // 256
    dp = D // 256
    kvp = list(range(qkp, 2 * qkp + avp))
    restp = [p_ for p_ in range(ws_ev_in.npan) if p_ not in kvp]
    phase_bgonly("wc0", conv_items(w_ev_in, 0, ws_ev_in, kvp))
    bg_kv = (conv_items(w_ev_in, 0, ws_ev_in, restp) + conv_items(w_ev_out, 0, ws_ev_out)
             + conv_items(w_gate, 0, ws_gate[0]) + conv_items(w_up, 0, ws_up[0]))
    bg_qc = conv_items(w_down, 0, ws_down[0])
    bg_conf = conv_items(w_od_in, 0, ws_od_in)
    bg_attn = (conv_items(w_od_out, 0, ws_od_out) + conv_items(w_gate, D, ws_gate[1])
               + conv_items(w_up, D, ws_up[1]) + conv_items(w_down, DFF, ws_down[1]))
    phase_inproj("kv", xseq, cfg.sblocks, 0, ws_ev_in, [
        {"kind": "single", "panels": list(range(qkp, 2 * qkp)), "out": KT, "dt": BF16},
        {"kind": "tok", "panels": list(range(2 * qkp, 2 * qkp + avp)), "out": VV},
    ], bg=bg_kv)
    phase_inproj("qc", xown, cfg.tblocks, 0, ws_ev_in, [
        {"kind": "single", "panels": list(range(0, qkp)), "out": QT, "dt": BF16},
        {"kind": "glu", "panels_a": list(range(2 * qkp + avp, 2 * qkp + avp + cwp)),
         "panels_b": list(range(2 * qkp + avp + cwp, 2 * qkp + avp + 2 * cwp)), "out": UU},
    ], bg=bg_qc)
    phase_conformer("conf", bg=bg_conf)
    phase_attention("attn", 0.8 - 0.6 * math.exp(-0.3 * 0), bg=bg_attn)
    phase_outproj("o0", [(AT, cfg.AW // 128), (UA, CW // 128)], ws_ev_out)
    phase_resid("r0", 1, xown, X1)

    def ffn(layer, xin, xout, final=False):
        phase_inproj(f"fg{layer}", xin, cfg.tblocks, layer * 4 + 2, ws_gate[layer], [
            {"kind": "single", "panels": list(range(fp)), "out": GG, "dt": F32}])
        phase_inproj(f"fu{layer}", xin, cfg.tblocks, layer * 4 + 2, ws_up[layer], [
            {"kind": "ffn_up", "panels": list(range(fp)), "out": HT, "layer": layer}])
        phase_outproj(f"fd{layer}", [(HT, FC)], ws_down[layer])
        phase_resid(f"fr{layer}", layer * 4 + 3, xin, xout, final=final)

    ffn(0, X1, X2)
    phase_inproj("od", X2, cfg.tblocks, 4, ws_od_in, [
        {"kind": "single", "panels": list(range(0, dp)), "out": GB, "dt": BF16},
        {"kind": "mul", "panels_a": list(range(dp, 2 * dp)), "panels_b": list(range(2 * dp, 3 * dp)), "out": PP},
    ])
    phase_odd_elt("oe")
    phase_outproj("o1", [(YT, KC)], ws_od_out)
    phase_resid("r1", 5, X2, X3)
    ffn(1, X3, yout, final=True)
    return nc


def host_inputs(cfg, inp):
    D, S, DFF, NH, TE, NT, NKT = cfg.D, cfg.S, cfg.DFF, cfg.NH, cfg.TE, cfg.NT, cfg.NKT
    CC = cfg.CW // 128
    FC = DFF // 128
    KC = cfg.KC
    f = lambda a: np.ascontiguousarray(np.asarray(a, dtype=np.float32))
    x = f(inp["x"])
    rel_bias = f(inp["rel_bias"])
    rep = lambda v: np.broadcast_to(f(v)[None, :], (128, f(v).shape[0]))
    shared = {
        "ev_w_in": f(inp["ev_w_in"])[0], "ev_w_out": f(inp["ev_w_out"])[0],
        "od_w_in": f(inp["od_w_in"])[0], "od_w_out": f(inp["od_w_out"])[0],
        "ffn_w_gate": f(inp["ffn_w_gate"]).reshape(2 * D, DFF),
        "ffn_w_up": f(inp["ffn_w_up"]).reshape(2 * D, DFF),
        "ffn_w_down": f(inp["ffn_w_down"]).reshape(2 * DFF, D),
    }
    gl = []
    for l in range(2):
        for nm in ("pre_mix_g", "post_mix_g", "pre_ffn_g", "post_ffn_g"):
            gl.append(rep(f(inp[nm])[l]))
    shared["grep"] = np.ascontiguousarray(np.concatenate(gl, 0))
    chunked = lambda v: np.ascontiguousarray(f(v).reshape(-1, 128).T)
    cw = f(inp["ev_conf_w"])[0]
    shared["conf_w"] = np.ascontiguousarray(cw.T.reshape(CC, 128, CONF_K).transpose(1, 0, 2).reshape(128, CC * CONF_K))
    shared["conf_v"] = np.ascontiguousarray(np.concatenate(
        [chunked(f(inp["ev_conf_b"])[0]), chunked(f(inp["ev_conf_ln_g"])[0]), chunked(f(inp["ev_conf_ln_b"])[0])], 1))
    ow = f(inp["od_conv_w"])[0]
    shared["od_cw"] = np.ascontiguousarray(ow.T.reshape(KC, 128, 3).transpose(1, 0, 2).reshape(128, KC * 3))
    fw_ = f(inp["ffn_conv_w"])
    shared["ffn_cw"] = np.ascontiguousarray(fw_.transpose(0, 2, 1).reshape(2, FC, 128, 3).transpose(2, 0, 1, 3).reshape(128, 2 * FC * 3))
    fb = f(inp["ffn_conv_b"])
    shared["ffn_cb"] = np.ascontiguousarray(fb.reshape(2, FC, 128).transpose(2, 0, 1).reshape(128, 2 * FC))
    shared["subg"] = np.ascontiguousarray(rep(f(inp["ev_subln_g"])[0]))
    shared["lamv"] = np.ascontiguousarray(np.concatenate(
        [rep(f(inp[k])[0]) for k in ("ev_lambda_q1", "ev_lambda_k1", "ev_lambda_q2", "ev_lambda_k2")], 1))
    ii = np.arange(128)[:, None]
    cc = np.arange(512 + 768)[None, :]
    idx = t5_bucket_np(ii - cc + 576)
    shared["wtoep"] = np.ascontiguousarray(np.concatenate([rel_bias[idx, h] for h in range(NH)], 0))
    shared["ident"] = np.eye(128, dtype=np.float32)
    left = rel_bias[t5_bucket_np(np.array(-100000)), :]
    right = rel_bias[t5_bucket_np(np.array(100000)), :]
    maps = []
    NQB = len(cfg.tblocks)
    for c in range(8):
        b, q = c // 4, c % 4
        a = q * cfg.TOWN
        xo = np.zeros((TE, D), np.float32)
        lo, hi = a - 64, a + cfg.TOWN + 64
        slo, shi = max(lo, 0), min(hi, S)
        xo[slo - lo:shi - lo] = x[b, slo:shi]
        msk = np.zeros((TE,), np.float32)
        msk[slo - lo:shi - lo] = 1.0
        rot = a - 256
        xs = np.roll(x[b], -rot, axis=0)
        tabs = np.arange(NKT) + rot // 128
        wrapped = (tabs < 0) | (tabs >= NKT)
        nm = np.where(wrapped, 0.0, 1.0).astype(np.float32)
        fcn = np.zeros((NKT, NH), np.float32)
        fcn[tabs < 0] = right
        fcn[tabs >= NKT] = left
        cf = np.zeros((NQB, NKT, NH), np.float32)
        for qb, (qs, qw) in enumerate(cfg.tblocks):
            for j in range(NKT):
                if tabs[j] < 0:
                    cf[qb, j] = right
                elif tabs[j] >= NKT:
                    cf[qb, j] = left
                else:
                    dj = 128 * j - cfg.QOFF - qs
                    cf[qb, j] = left if dj < 0 else right
        m = dict(shared)
        m["xown"] = xo
        m["xseq"] = np.ascontiguousarray(xs)
        m["mask"] = np.ascontiguousarray(msk.reshape(NT, 128).T)
        m["cfar"] = np.ascontiguousarray(np.broadcast_to(cf.reshape(1, -1), (128, NQB * NKT * NH)))
        m["nm"] = np.ascontiguousarray(np.broadcast_to(nm.reshape(1, -1), (128, NKT)))
        m["fcn"] = np.ascontiguousarray(np.broadcast_to(fcn.reshape(1, -1), (128, NKT * NH)))
        maps.append(m)
    return maps


def run(cfg, inp, dbg=()):
    nc = build_program(cfg, dbg)
    maps = host_inputs(cfg, inp)
    res = run_bass_kernel_spmd(nc, maps, core_ids=list(range(8)))
    out = np.zeros((2, cfg.S, cfg.D), np.float32)
    for c in range(8):
        b, q = c // 4, c % 4
        out[b, q * cfg.TOWN:(q + 1) * cfg.TOWN] = res.results[c]["y"]
    return out, res


def kernel(**inputs):
    cfg = Cfg(4096, 8192, 11008)
    out, _ = run(cfg, inputs)
    return out
```

```python
import contextlib
import math
import numpy as np
import concourse.bass as bass
import concourse.mybir as mybir
from concourse.bass_utils import run_bass_kernel_spmd

F32 = mybir.dt.float32
BF16 = mybir.dt.bfloat16
ALU = mybir.AluOpType
AF = mybir.ActivationFunctionType
ENGINES = ("sync", "gpsimd", "scalar", "vector", "tensor")
EPS = 1e-6
PH_LIMIT = [999]
ATT_CUT = [9]
N_BUCKETS = 32
MAX_DISTANCE = 128
CONF_K = 31


class Buf:
    __slots__ = ("name", "t", "last_w", "readers", "dsem")

    def __init__(self, name, t=None):
        self.name = name
        self.t = t
        self.last_w = None
        self.readers = []
        self.dsem = None


def I(m, **kw):
    return (m, kw)


class Phase:
    def __init__(self, nc, name):
        self.nc = nc
        self.name = name
        self.stack = contextlib.ExitStack()
        self.cm = nc.cleanup_on_exit()
        self.cm.__enter__()
        self.ops = {e: [] for e in ENGINES}
        self.sems = {}
        self.counts = {}
        self.nbuf = 0
        self.rr = 0
        self.bg = []

    def sem(self, key):
        if key not in self.sems:
            self.sems[key] = self.nc.alloc_semaphore(f"{self.name}_{key}")
            self.counts[key] = 0
        return self.sems[key]

    def sbuf(self, name, shape, dtype):
        t = self.stack.enter_context(self.nc.sbuf_tensor(f"{self.name}_{name}", list(shape), dtype))
        return Buf(name, t)

    def psum(self, name, shape, dtype=F32):
        t = self.stack.enter_context(self.nc.psum_tensor(f"{self.name}_{name}", list(shape), dtype))
        return Buf(name, t)

    def _deps(self, reads, writes):
        waits = []
        for b in reads:
            if b.last_w is not None:
                waits.append(b.last_w)
        for b in writes:
            if b.last_w is not None:
                waits.append(b.last_w)
            waits.extend(b.readers)
        return waits

    def _commit(self, tok, reads, writes):
        for b in reads:
            b.readers.append(tok)
        for b in writes:
            b.last_w = tok
            b.readers = []

    def op(self, eng, fn, reads=(), writes=()):
        if isinstance(fn, tuple):
            fn = [fn]
        waits = self._deps(reads, writes)
        if eng == "tensor":
            waits = [w_ for w_ in waits if w_[0] != "e_tensor"]
        key = "e_" + eng
        self.sem(key)
        self.counts[key] += 1
        tok = (key, self.counts[key])
        self.ops[eng].append((waits, fn, key, 1))
        self._commit(tok, reads, writes)
        return tok

    def dma(self, eng, out, in_, reads=(), writes=()):
        waits = self._deps(reads, writes)
        bufs = list(writes) + list(reads)
        b = bufs[0]
        if b.dsem is None:
            self.nbuf += 1
            b.dsem = f"d{self.nbuf}"
        key = b.dsem
        self.sem(key)
        self.counts[key] += 16
        tok = (key, self.counts[key])

        self.ops[eng].append((waits, [("dma_start", dict(out=out, in_=in_))], key, 16))
        self._commit(tok, reads, writes)
        return tok

    def load(self, out, in_, writes):
        return self.dma("sync", out, in_, writes=writes)

    def store(self, out, in_, reads):
        return self.dma("gpsimd", out, in_, reads=reads)

    def add_bg(self, items):
        self.bg.extend(items)

    def _merge_bg(self):
        if not self.bg:
            return
        self.sem("bg")
        g = self.ops["gpsimd"]
        n, m = len(g), len(self.bg)
        pos = [bi * n // m for bi in range(m)]
        merged = []
        bi = 0
        for i in range(n + 1):
            while bi < m and pos[bi] <= i:
                out, in_ = self.bg[bi]
                self.counts["bg"] += 16
                merged.append(([], [("dma_start", dict(out=out, in_=in_))], "bg", 16))
                bi += 1
            if i < n:
                merged.append(g[i])
        self.ops["gpsimd"] = merged

    def run(self):
        nc = self.nc
        self._merge_bg()
        final_dma = {k: v for k, v in self.counts.items() if not k.startswith("e_")}
        with nc.Block() as block:
            for eng in ENGINES:
                oplist = self.ops[eng]
                if not oplist and eng != "gpsimd":
                    continue

                def body(e, oplist=oplist, eng=eng):
                    waited = {}
                    for waits, fn, key, inc in oplist:
                        for (wk, wv) in waits:
                            if waited.get(wk, 0) >= wv:
                                continue
                            e.wait_ge(self.sems[wk], wv)
                            waited[wk] = wv
                        for (mname, kw) in fn:
                            ins = getattr(e, mname)(**kw)
                        ins.then_inc(self.sems[key], inc)
                    if eng == "gpsimd":
                        for k, v in final_dma.items():
                            if v > 0 and waited.get(k, 0) < v:
                                e.wait_ge(self.sems[k], v)
                        for k in ("e_sync", "e_scalar", "e_vector", "e_tensor"):
                            if self.counts.get(k, 0) > 0:
                                e.wait_ge(self.sems[k], self.counts[k])
                getattr(block, eng)(body)
        self.stack.close()
        self.cm.__exit__(None, None, None)


class Cfg:
    def __init__(self, D=4096, S=8192, DFF=11008):
        self.D, self.S, self.DFF = D, S, DFF
        self.KC = D // 128
        self.AW = D // 2
        self.CW = D // 2
        self.NH = self.AW // 256
        self.QK = self.NH * 256
        self.WIN = 2 * self.QK + self.AW + 2 * self.CW
        self.TOWN = S // 4
        self.TE = self.TOWN + 128
        self.NT = self.TE // 128
        self.NKT = S // 128
        self.tblocks = []
        t = 0
        while t < self.TE:
            rem = self.TE - t
            if rem == 640:
                w = 384
            else:
                w = min(512, rem)
            self.tblocks.append((t, w))
            t += w
        self.sblocks = [(t, 512) for t in range(0, S, 512)]
        self.QOFF = 192

    def near(self, qs, qw, j):
        dj = 128 * j - self.QOFF - qs
        return (-218 < dj < qw + 90), dj


def t5_bucket_np(rel):
    nb = N_BUCKETS // 2
    max_exact = nb // 2
    ret = np.where(rel > 0, nb, 0)
    n = np.abs(rel)
    nf = np.maximum(n, max_exact).astype(np.float32)
    large = max_exact + (np.log(nf / max_exact) / math.log(MAX_DISTANCE / max_exact)
                         * (nb - max_exact)).astype(np.int32)
    large = np.minimum(large, nb - 1)
    return ret + np.where(n < max_exact, n, large)


def kblocks_of(kc_total, kcb):
    out = []
    k = 0
    while k < kc_total:
        n = min(kcb, kc_total - k)
        out.append((k, n))
        k += n
    return out


class WScr:
    def __init__(self, nc, name, K, N, npw, kcb):
        self.K, self.N, self.npw = K, N, npw
        self.kbs = kblocks_of(K // 128, kcb)
        self.npan = N // npw
        self.t = {}
        for p in range(self.npan):
            for bi, (k0, n) in enumerate(self.kbs):
                self.t[(p, bi)] = nc.dram_tensor(f"ws_{name}_{p}_{bi}", [128, n * npw], BF16).ap()


def build_program(cfg, dbg=()):
    nc = bass.Bass("TRN2", target_bir_lowering=False)
    D, S, DFF, KC, NH, CW, TE, NT = cfg.D, cfg.S, cfg.DFF, cfg.KC, cfg.NH, cfg.CW, cfg.TE, cfg.NT
    CC = CW // 128
    FC = DFF // 128

    def din(name, shape, dt=F32):
        return nc.dram_tensor(name, list(shape), dt, kind="ExternalInput").ap()

    def scr(name, shape, dt):
        if name in dbg:
            return nc.dram_tensor(name, list(shape), dt, kind="ExternalOutput").ap()
        return nc.dram_tensor(name, list(shape), dt).ap()

    xown = din("xown", [TE, D])
    xseq = din("xseq", [S, D])
    maskd = din("mask", [128, NT])
    w_ev_in = din("ev_w_in", [D, cfg.WIN])
    w_ev_out = din("ev_w_out", [D, D])
    w_od_in = din("od_w_in", [D, 3 * D])
    w_od_out = din("od_w_out", [D, D])
    w_gate = din("ffn_w_gate", [2 * D, DFF])
    w_up = din("ffn_w_up", [2 * D, DFF])
    w_down = din("ffn_w_down", [2 * DFF, D])
    grep = din("grep", [8 * 128, D])
    conf_w = din("conf_w", [128, CC * CONF_K])
    conf_v = din("conf_v", [128, 3 * CC])
    od_cw = din("od_cw", [128, KC * 3])
    ffn_cw = din("ffn_cw", [128, 2 * FC * 3])
    ffn_cb = din("ffn_cb", [128, 2 * FC])
    subg = din("subg", [128, 256])
    lamv = din("lamv", [128, 4 * 128])
    wtoep_w = 512 + 768
    wtoep = din("wtoep", [NH * 128, wtoep_w])
    NQB = len(cfg.tblocks)
    cfar = din("cfar", [128, NQB * cfg.NKT * NH])
    nmd = din("nm", [128, cfg.NKT])
    fcnd = din("fcn", [128, cfg.NKT * NH])
    identd = din("ident", [128, 128])
    yout = nc.dram_tensor("y", [cfg.TOWN, D], F32, kind="ExternalOutput").ap()

    ws_ev_in = WScr(nc, "evin", D, cfg.WIN, 256, KC)
    ws_ev_out = WScr(nc, "evout", D, D, 512, 16)
    ws_od_in = WScr(nc, "odin", D, 3 * D, 256, KC)
    ws_od_out = WScr(nc, "odout", D, D, 512, 16)
    ws_gate = [WScr(nc, f"gate{l}", D, DFF, 256, KC) for l in range(2)]
    ws_up = [WScr(nc, f"up{l}", D, DFF, 256, KC) for l in range(2)]
    ws_down = [WScr(nc, f"down{l}", DFF, D, 512, 16) for l in range(2)]
    QT = scr("QT", [cfg.QK, TE], BF16)
    KT = scr("KT", [cfg.QK, S], BF16)
    VV = scr("VV", [S, cfg.AW], BF16)
    UU = scr("UU", [CW, TE], BF16)
    UA = scr("UA", [CW, TE], BF16)
    AT = scr("AT", [cfg.AW, TE], BF16)
    MM = scr("MM", [TE, D], F32)
    SS = scr("SS", [128, NT * (D // 512)], F32)
    X1 = scr("X1", [TE, D], F32)
    X2 = scr("X2", [TE, D], F32)
    X3 = scr("X3", [TE, D], F32)
    GG = scr("GG", [DFF, TE], F32)
    UP = scr("UP", [DFF, TE], BF16)
    HT = scr("HT", [DFF, TE], BF16)
    GB = scr("GB", [D, TE], BF16)
    PP = scr("PP", [D, TE], F32)
    YT = scr("YT", [D, TE], BF16)

    alt = [0]
    phc = [0]

    def skip():
        phc[0] += 1
        return phc[0] > PH_LIMIT[0]

    def alt_eng():
        alt[0] ^= 1
        return "vector" if alt[0] else "scalar"

    def copy_op(ph, eng, out_ap, in_ap, reads, writes):
        if eng == "vector":
            ph.op("vector", I("tensor_copy", out=out_ap, in_=in_ap), reads=reads, writes=writes)
        else:
            ph.op("scalar", I("copy", out=out_ap, in_=in_ap), reads=reads, writes=writes)

    def conv_items(src, row0, ws, panels=None):
        items = []
        npw = ws.npw
        for p in (range(ws.npan) if panels is None else panels):
            for bi, (k0, n) in enumerate(ws.kbs):
                for c0 in range(0, n, 16):
                    nn = min(16, n - c0)
                    r0 = row0 + (k0 + c0) * 128
                    items.append((ws.t[(p, bi)][:, c0 * npw:(c0 + nn) * npw].rearrange("p (k n) -> p k n", n=npw),
                                  src[r0:r0 + nn * 128, p * npw:(p + 1) * npw].rearrange("(k p) n -> p k n", p=128)))
        return items

    def phase_bgonly(name, items):
        if skip():
            return
        ph = Phase(nc, name)
        ph.add_bg(items)
        ph.run()

    def phase_wconv(name, items):
        if skip():
            return
        ph = Phase(nc, name)
        NS = 3
        st32 = [ph.sbuf(f"s32_{i}", [128, 4096], F32) for i in range(NS)]
        st16 = [ph.sbuf(f"s16_{i}", [128, 4096], BF16) for i in range(NS)]
        u = 0
        for (src, row0, ws) in items:
            npw = ws.npw
            kcs_max = 4096 // npw
            for p in range(ws.npan):
                for bi, (k0, n) in enumerate(ws.kbs):
                    c = 0
                    while c < n:
                        kcs = min(kcs_max, n - c)
                        a, b = st32[u % NS], st16[u % NS]
                        r0 = row0 + (k0 + c) * 128
                        sap = src[r0:r0 + kcs * 128, p * npw:(p + 1) * npw].rearrange("(k p) n -> p k n", p=128)
                        ph.load(a.t[:, 0:kcs * npw].rearrange("p (k n) -> p k n", n=npw), sap, writes=[a])
                        copy_op(ph, alt_eng(), b.t[:, 0:kcs * npw], a.t[:, 0:kcs * npw], [a], [b])
                        ph.store(ws.t[(p, bi)][:, c * npw:(c + kcs) * npw], b.t[:, 0:kcs * npw], reads=[b])
                        c += kcs
                        u += 1
        ph.run()

    def phase_inproj(name, xsrc, blocks, g_idx, ws, jobs, bg=(), resid=None):
        if skip():
            return
        ph = Phase(nc, name)
        ph.add_bg(bg)
        if resid is not None:
            NPr = D // 512
            gpost = ph.sbuf("gpost", [128, D], F32)
            sspr = ph.sbuf("sspr", [128, NT * NPr], F32)
            mskr = ph.sbuf("mskr", [128, NT], F32)
            mtr = ph.sbuf("mtr", [128, D], F32)
            s2r = [ph.sbuf(f"s2r{i}", [128, 1], F32) for i in range(2)]
            ph.load(gpost.t[:], grep[resid["g_idx"] * 128:(resid["g_idx"] + 1) * 128, :], writes=[gpost])
            ph.load(sspr.t[:], SS[:, :], writes=[sspr])
            ph.load(mskr.t[:], maskd[:, :], writes=[mskr])
        ident32 = ph.sbuf("id32", [128, 128], F32)
        identb = ph.sbuf("idb", [128, 128], BF16)
        grp = ph.sbuf("grp", [128, D], F32)
        epsb = ph.sbuf("epsb", [128, 1], F32)
        ph.load(ident32.t[:], identd[:, :], writes=[ident32])
        ph.op("vector", I("tensor_copy", out=identb.t[:], in_=ident32.t[:]), reads=[ident32], writes=[identb])
        ph.load(grp.t[:], grep[g_idx * 128:(g_idx + 1) * 128, :], writes=[grp])
        ph.op("vector", I("memset", ap=epsb.t[:], constant=EPS), writes=[epsb])
        xt = [ph.sbuf(f"xt{i}", [128, D], F32) for i in range(2)]
        xs = [ph.sbuf(f"xs{i}", [128, D], BF16) for i in range(2)]
        junk = ph.sbuf("junk", [128, D], BF16)
        ssq = [ph.sbuf(f"ssq{i}", [128, 1], F32) for i in range(2)]
        rst = [ph.sbuf(f"rst{i}", [128, 1], F32) for i in range(2)]
        hnT = ph.sbuf("hnT", [128, KC, 512], BF16)
        TB = min(8, KC)
        ptr = [ph.psum(f"ptr{i}", [128, 1024], BF16) for i in range(2)]
        pacc = [ph.psum(f"pacc{i}", [128, 512], F32) for i in range(4)]
        NWS = 3
        wsl = [ph.sbuf(f"w{i}", [128, KC * 256], BF16) for i in range(NWS)]
        stg = [ph.sbuf(f"stg{i}", [128, 512], F32) for i in range(4)]
        stgb = [ph.sbuf(f"stgb{i}", [128, 512], BF16) for i in range(4)]
        sgt = [ph.sbuf(f"sg{i}", [128, 512], F32) for i in range(2)]
        cnt = {"x": 0, "tr": 0, "w": 0, "acc": 0, "stg": 0, "sg": 0, "gb": 0}
        has_ffn = any(jb["kind"] == "ffn_up" for jb in jobs)
        if has_ffn:
            fcw = ph.sbuf("fcw", [128, 2 * FC * 3], F32)
            fcb = ph.sbuf("fcb", [128, 2 * FC], F32)
            ph.load(fcw.t[:], ffn_cw[:, :], writes=[fcw])
            ph.load(fcb.t[:], ffn_cb[:, :], writes=[fcb])
            gbuf = [ph.sbuf(f"gbuf{i}", [128, 514], F32) for i in range(3)]
            abuf = [ph.sbuf(f"abuf{i}", [128, 512], F32) for i in range(3)]

        def wload(pi):
            sl = wsl[cnt["w"] % NWS]
            cnt["w"] += 1
            ph.load(sl.t[:], ws.t[(pi, 0)][:, :], writes=[sl])
            return sl

        def accgroup(ps, sl, j, w):
            fn = [I("matmul", out=ps.t[:, 0:w], lhsT=sl.t[:, kc * 256 + j * 128: kc * 256 + (j + 1) * 128],
                    rhs=hnT.t[:, kc, 0:w], start=(kc == 0), stop=(kc == KC - 1)) for kc in range(KC)]
            ph.op("tensor", fn, reads=[sl, hnT], writes=[ps])

        for (tok0, w) in blocks:
            nt = w // 128
            for t in range(nt):
                i = cnt["x"] % 2
                cnt["x"] += 1
                a, b, s1, r1 = xt[i], xs[i], ssq[i], rst[i]
                if resid is None:
                    ph.load(a.t[:], xsrc[tok0 + t * 128: tok0 + (t + 1) * 128, :], writes=[a])
                else:
                    ti = tok0 // 128 + t
                    s2 = s2r[i]
                    ph.load(mtr.t[:], MM[ti * 128:(ti + 1) * 128, :], writes=[mtr])
                    ph.load(a.t[:], resid["xin"][ti * 128:(ti + 1) * 128, :], writes=[a])
                    ph.op("vector", I("reduce_sum", out=s2.t[:], in_=sspr.t[:, ti * NPr:(ti + 1) * NPr], axis=mybir.AxisListType.X),
                          reads=[sspr], writes=[s2])
                    ph.op("scalar", I("activation", out=s2.t[:], in_=s2.t[:], func=AF.Sqrt, scale=1.0 / D, bias=epsb.t[:, 0:1]),
                          reads=[s2, epsb], writes=[s2])
                    ph.op("vector", I("reciprocal", out=s2.t[:], in_=s2.t[:]), reads=[s2], writes=[s2])
                    ph.op("vector", I("tensor_tensor", out=s2.t[:], in0=s2.t[:], in1=mskr.t[:, ti:ti + 1], op=ALU.mult),
                          reads=[s2, mskr], writes=[s2])
                    ph.op("vector", I("tensor_tensor", out=mtr.t[:], in0=mtr.t[:], in1=gpost.t[:], op=ALU.mult), reads=[mtr, gpost], writes=[mtr])
                    ph.op("vector", I("scalar_tensor_tensor", out=a.t[:], in0=mtr.t[:], scalar=s2.t[:, 0:1], in1=a.t[:],
                                      op0=ALU.mult, op1=ALU.add), reads=[mtr, s2, a], writes=[a])
                    ph.store(resid["xout"][ti * 128:(ti + 1) * 128, :], a.t[:], reads=[a])
                ph.op("scalar", I("activation", out=junk.t[:], in_=a.t[:], func=AF.Square, accum_out=s1.t[:]),
                      reads=[a], writes=[junk, s1])
                ph.op("scalar", I("activation", out=r1.t[:], in_=s1.t[:], func=AF.Sqrt, scale=1.0 / D, bias=epsb.t[:, 0:1]),
                      reads=[s1, epsb], writes=[r1])
                ph.op("vector", I("reciprocal", out=r1.t[:], in_=r1.t[:]), reads=[r1], writes=[r1])
                ph.op("vector", I("scalar_tensor_tensor", out=b.t[:], in0=a.t[:], scalar=r1.t[:, 0:1], in1=grp.t[:],
                                                                                 op0=ALU.mult, op1=ALU.mult),
                      reads=[a, r1, grp], writes=[b])
                for c0 in range(0, KC, TB):
                    pt = ptr[cnt["tr"] % 2]
                    cnt["tr"] += 1

                    ftr = [I("transpose", out=pt.t[:, c * 128:(c + 1) * 128], in_=b.t[:, (c0 + c) * 128:(c0 + c + 1) * 128],
                             identity=identb.t[:]) for c in range(TB)]
                    ph.op("tensor", ftr, reads=[b, identb], writes=[pt])
                    copy_op(ph, alt_eng(), hnT.t[:, c0:c0 + TB, t * 128:(t + 1) * 128],
                            pt.t[:, 0:TB * 128].rearrange("p (c k) -> p c k", k=128), [pt], [hnT])
            for job in jobs:
                kind = job["kind"]
                if kind == "single":
                    for pi in job["panels"]:
                        sl = wload(pi)
                        for j in range(2):
                            ps = pacc[cnt["acc"] % 4]
                            cnt["acc"] += 1
                            accgroup(ps, sl, j, w)
                            row = (pi - job["panels"][0]) * 256 + j * 128
                            k = cnt["stg"] % 4
                            cnt["stg"] += 1
                            if job["dt"] == BF16:
                                sb = stgb[k]
                            else:
                                sb = stg[k]
                            copy_op(ph, alt_eng(), sb.t[:, 0:w], ps.t[:, 0:w], [ps], [sb])
                            ph.store(job["out"][row:row + 128, tok0:tok0 + w], sb.t[:, 0:w], reads=[sb])
                elif kind in ("glu", "mul"):
                    for pa, pb in zip(job["panels_a"], job["panels_b"]):
                        sla = wload(pa)
                        slb = wload(pb)
                        for j in range(2):
                            psa = pacc[cnt["acc"] % 4]
                            psb = pacc[(cnt["acc"] + 1) % 4]
                            cnt["acc"] += 2
                            accgroup(psa, sla, j, w)
                            accgroup(psb, slb, j, w)
                            row = (pa - job["panels_a"][0]) * 256 + j * 128
                            sg = sgt[cnt["sg"] % 2]
                            cnt["sg"] += 1
                            k = cnt["stg"] % 4
                            cnt["stg"] += 1
                            sb = stgb[k] if kind == "glu" else stg[k]
                            fnc = AF.Sigmoid if kind == "glu" else AF.Copy
                            ph.op("scalar", I("activation", out=sg.t[:, 0:w], in_=psb.t[:, 0:w], func=fnc),
                                  reads=[psb], writes=[sg])
                            ph.op("vector", I("tensor_tensor", out=sb.t[:, 0:w], in0=psa.t[:, 0:w], in1=sg.t[:, 0:w], op=ALU.mult),
                                  reads=[psa, sg], writes=[sb])
                            ph.store(job["out"][row:row + 128, tok0:tok0 + w], sb.t[:, 0:w], reads=[sb])
                elif kind == "ffn_up":
                    layer = job["layer"]
                    for pi in job["panels"]:
                        sl = wload(pi)
                        for j in range(2):
                            c = pi * 2 + j
                            gbf = gbuf[cnt["gb"] % 3]
                            ab = abuf[cnt["gb"] % 3]
                            cnt["gb"] += 1
                            lo = max(tok0 - 1, 0)
                            hi = min(tok0 + w + 1, TE)
                            if lo > tok0 - 1 or hi < tok0 + w + 1:
                                ph.op("vector", I("memset", ap=gbf.t[:], constant=0.0), writes=[gbf])
                            ph.load(gbf.t[:, lo - (tok0 - 1):hi - (tok0 - 1)], GG[c * 128:(c + 1) * 128, lo:hi], writes=[gbf])
                            wb = (layer * FC + c) * 3
                            bb = layer * FC + c
                            fconv = [I("tensor_scalar", out=ab.t[:, 0:w], in0=gbf.t[:, 0:w], scalar1=fcw.t[:, wb:wb + 1], scalar2=fcb.t[:, bb:bb + 1], op0=ALU.mult, op1=ALU.add),
                                     I("scalar_tensor_tensor", out=ab.t[:, 0:w], in0=gbf.t[:, 1:w + 1], scalar=fcw.t[:, wb + 1:wb + 2], in1=ab.t[:, 0:w], op0=ALU.mult, op1=ALU.add),
                                     I("scalar_tensor_tensor", out=ab.t[:, 0:w], in0=gbf.t[:, 2:w + 2], scalar=fcw.t[:, wb + 2:wb + 3], in1=ab.t[:, 0:w], op0=ALU.mult, op1=ALU.add)]
                            ph.op("vector", fconv, reads=[gbf, fcw, fcb], writes=[ab])
                            ph.op("scalar", I("activation", out=ab.t[:, 0:w], in_=ab.t[:, 0:w], func=AF.Gelu_apprx_tanh), reads=[ab], writes=[ab])
                            ps = pacc[cnt["acc"] % 4]
                            cnt["acc"] += 1
                            accgroup(ps, sl, j, w)
                            k = cnt["stg"] % 4
                            cnt["stg"] += 1
                            sb = stgb[k]
                            ph.op("vector", I("tensor_tensor", out=sb.t[:, 0:w], in0=ps.t[:, 0:w], in1=ab.t[:, 0:w], op=ALU.mult), reads=[ps, ab], writes=[sb])
                            ph.store(job["out"][c * 128:(c + 1) * 128, tok0:tok0 + w], sb.t[:, 0:w], reads=[sb])
                elif kind == "tok":
                    for pi in job["panels"]:
                        sl = wload(pi)
                        for t in range(nt):
                            ps = pacc[cnt["acc"] % 4]
                            cnt["acc"] += 1

                            fn = [I("matmul", out=ps.t[:, 0:256], lhsT=hnT.t[:, kc, t * 128:(t + 1) * 128],
                                    rhs=sl.t[:, kc * 256:(kc + 1) * 256], start=(kc == 0), stop=(kc == KC - 1)) for kc in range(KC)]
                            ph.op("tensor", fn, reads=[sl, hnT], writes=[ps])
                            k = cnt["stg"] % 4
                            cnt["stg"] += 1
                            sb = stgb[k]
                            copy_op(ph, alt_eng(), sb.t[:, 0:256], ps.t[:, 0:256], [ps], [sb])
                            col = (pi - job["panels"][0]) * 256
                            ph.store(job["out"][tok0 + t * 128: tok0 + (t + 1) * 128, col:col + 256], sb.t[:, 0:256], reads=[sb])
        ph.run()

    def phase_outproj(name, srcs, ws):
        if skip():
            return
        ph = Phase(nc, name)
        kct = sum(k for _, k in srcs)
        NP = D // 512
        NWS = 3
        L = ph.sbuf("L", [128, kct, 512], BF16)
        wsl = [ph.sbuf(f"w{i}", [128, 16 * 512], BF16) for i in range(NWS)]
        pacc = [ph.psum(f"pacc{i}", [128, 512], F32) for i in range(8)]
        stg = [ph.sbuf(f"stg{i}", [128, 512], F32) for i in range(4)]
        junk = ph.sbuf("junk", [128, 512], BF16)
        ssp = ph.sbuf("ssp", [128, NT * NP], F32)
        ph.op("vector", I("memset", ap=ssp.t[:], constant=0.0), writes=[ssp])
        cnt = {"w": 0, "stg": 0, "g": 0}
        for (tok0, w) in cfg.tblocks:
            nt = w // 128
            kk = 0
            for (src, kcs) in srcs:
                for c0 in range(0, kcs, 16):
                    n = min(16, kcs - c0)
                    ph.load(L.t[:, kk + c0: kk + c0 + n, 0:w],
                            src[c0 * 128:(c0 + n) * 128, tok0:tok0 + w].rearrange("(k p) t -> p k t", p=128), writes=[L])
                kk += kcs
            for pn in range(NP):
                accs = pacc[(cnt["g"] % 2) * 4:(cnt["g"] % 2) * 4 + 4]
                cnt["g"] += 1
                nkb = len(ws.kbs)
                for bi, (k0, n) in enumerate(ws.kbs):
                    sl = wsl[cnt["w"] % NWS]
                    cnt["w"] += 1
                    ph.load(sl.t[:, 0:n * 512], ws.t[(pn, bi)][:, :], writes=[sl])
                    for t in range(nt):
                        fn = [I("matmul", out=accs[t].t[:, :], lhsT=L.t[:, k0 + c, t * 128:(t + 1) * 128], rhs=sl.t[:, c * 512:(c + 1) * 512],
                                start=(bi == 0 and c == 0), stop=(bi == nkb - 1 and c == n - 1)) for c in range(n)]
                        ph.op("tensor", fn, reads=[sl, L], writes=[accs[t]])
                for t in range(nt):
                    tile_i = tok0 // 128 + t
                    sb = stg[cnt["stg"] % 4]
                    cnt["stg"] += 1
                    acc = accs[t]
                    ph.op("vector", I("tensor_copy", out=sb.t[:], in_=acc.t[:]), reads=[acc], writes=[sb])
                    col = tile_i * NP + pn
                    ph.op("scalar", I("activation", out=junk.t[:], in_=acc.t[:], func=AF.Square, accum_out=ssp.t[:, col:col + 1]),
                          reads=[], writes=[junk, ssp, acc])
                    ph.store(MM[tile_i * 128:(tile_i + 1) * 128, pn * 512:(pn + 1) * 512], sb.t[:], reads=[sb])
        ph.store(SS[:, :], ssp.t[:], reads=[ssp])
        ph.run()

    def phase_resid(name, g_idx, xin, xout, final=False):
        if skip():
            return
        ph = Phase(nc, name)
        NP = D // 512
        grp = ph.sbuf("grp", [128, D], F32)
        ssp = ph.sbuf("ssp", [128, NT * NP], F32)
        msk = ph.sbuf("msk", [128, NT], F32)
        epsb = ph.sbuf("epsb", [128, 1], F32)
        ph.load(grp.t[:], grep[g_idx * 128:(g_idx + 1) * 128, :], writes=[grp])
        ph.load(ssp.t[:], SS[:, :], writes=[ssp])
        ph.load(msk.t[:], maskd[:, :], writes=[msk])
        ph.op("vector", I("memset", ap=epsb.t[:], constant=EPS), writes=[epsb])
        mt = [ph.sbuf(f"m{i}", [128, D], F32) for i in range(2)]
        xt = [ph.sbuf(f"x{i}", [128, D], F32) for i in range(2)]
        ss1 = [ph.sbuf(f"ss{i}", [128, 1], F32) for i in range(2)]
        tiles = range(NT)
        for n, ti in enumerate(tiles):
            if final and (ti * 128 + 128 <= 64 or ti * 128 >= 64 + cfg.TOWN):
                pass
            m, x, s1 = mt[n % 2], xt[n % 2], ss1[n % 2]
            ph.load(m.t[:], MM[ti * 128:(ti + 1) * 128, :], writes=[m])
            ph.load(x.t[:], xin[ti * 128:(ti + 1) * 128, :], writes=[x])
            ph.op("vector", I("reduce_sum", out=s1.t[:], in_=ssp.t[:, ti * NP:(ti + 1) * NP], axis=mybir.AxisListType.X),
                  reads=[ssp], writes=[s1])
            ph.op("scalar", I("activation", out=s1.t[:], in_=s1.t[:], func=AF.Sqrt, scale=1.0 / D, bias=epsb.t[:, 0:1]),
                  reads=[s1, epsb], writes=[s1])
            ph.op("vector", I("reciprocal", out=s1.t[:], in_=s1.t[:]), reads=[s1], writes=[s1])
            ph.op("vector", I("tensor_tensor", out=s1.t[:], in0=s1.t[:], in1=msk.t[:, ti:ti + 1], op=ALU.mult),
                  reads=[s1, msk], writes=[s1])
            ph.op("vector", I("tensor_tensor", out=m.t[:], in0=m.t[:], in1=grp.t[:], op=ALU.mult), reads=[m, grp], writes=[m])
            ph.op("vector", I("scalar_tensor_tensor", out=x.t[:], in0=m.t[:], scalar=s1.t[:, 0:1], in1=x.t[:],
                                                                             op0=ALU.mult, op1=ALU.add),
                  reads=[m, s1, x], writes=[x])
            if not final:
                ph.store(xout[ti * 128:(ti + 1) * 128, :], x.t[:], reads=[x])
            else:
                lo = max(ti * 128, 64)
                hi = min(ti * 128 + 128, 64 + cfg.TOWN)
                if hi > lo:
                    ph.store(xout[lo - 64:hi - 64, :], x.t[lo - ti * 128:hi - ti * 128, :], reads=[x])
        ph.run()

    def phase_conformer(name, bg=()):
        if skip():
            return
        ph = Phase(nc, name)
        ph.add_bg(bg)
        PAD = (CONF_K - 1) // 2
        cw = ph.sbuf("cw", [128, CC * CONF_K], F32)
        cv = ph.sbuf("cv", [128, 3 * CC], F32)
        ones = ph.sbuf("ones", [128, 128], F32)
        ident32 = ph.sbuf("id32", [128, 128], F32)
        ph.load(cw.t[:], conf_w[:, :], writes=[cw])
        ph.load(cv.t[:], conf_v[:, :], writes=[cv])
        ph.load(ident32.t[:], identd[:, :], writes=[ident32])
        ph.op("vector", I("memset", ap=ones.t[:], constant=1.0), writes=[ones])
        dg = ph.sbuf("dg", [128, CC * CONF_K, 128], BF16)
        fdg = [I("tensor_scalar", out=dg.t[:, i, :], in0=ident32.t[:], scalar1=cw.t[:, i:i + 1], scalar2=None, op0=ALU.mult)
               for i in range(CC * CONF_K)]
        ph.op("vector", fdg, reads=[ident32, cw], writes=[dg])
        uin = [ph.sbuf(f"uin{i}", [128, 512 + 2 * PAD], BF16) for i in range(3)]
        cout = [ph.sbuf(f"co{i}", [128, 512], F32) for i in range(CC)]
        sq = [ph.sbuf(f"sq{i}", [128, 512], F32) for i in range(3)]
        pconv = [ph.psum(f"pcv{i}", [128, 512], F32) for i in range(2)]
        psum_s = ph.psum("ps_s", [128, 512], F32)
        psum_q = ph.psum("ps_q", [128, 512], F32)
        mean = ph.sbuf("mean", [128, 512], F32)
        rstd = ph.sbuf("rstd", [128, 512], F32)
        epsb = ph.sbuf("epsb", [128, 1], F32)
        ph.op("vector", I("memset", ap=epsb.t[:], constant=EPS), writes=[epsb])
        ob = [ph.sbuf(f"ob{i}", [128, 512], BF16) for i in range(3)]
        n_u = 0
        pend = []

        def emit_stats(c, co, s, w):
            ph.op("tensor", I("matmul", out=psum_s.t[:, 0:w], lhsT=ones.t[:], rhs=co.t[:, 0:w], start=(c == 0), stop=(c == CC - 1)),
                  reads=[ones, co], writes=[psum_s])
            ph.op("tensor", I("matmul", out=psum_q.t[:, 0:w], lhsT=ones.t[:], rhs=s.t[:, 0:w], start=(c == 0), stop=(c == CC - 1)),
                  reads=[ones, s], writes=[psum_q])

        for (tok0, w) in cfg.tblocks:
            lo = max(tok0 - PAD, 0)
            hi = min(tok0 + w + PAD, TE)
            for c in range(CC):
                ui = uin[n_u % 3]
                pc = pconv[n_u % 2]
                n_u += 1
                if lo > tok0 - PAD or hi < tok0 + w + PAD:
                    ph.op("vector", I("memset", ap=ui.t[:], constant=0.0), writes=[ui])
                ph.load(ui.t[:, lo - (tok0 - PAD): hi - (tok0 - PAD)], UU[c * 128:(c + 1) * 128, lo:hi], writes=[ui])
                co = cout[c]
                fconv = [I("matmul", out=pc.t[:, 0:w], lhsT=dg.t[:, c * CONF_K + j, :], rhs=ui.t[:, j:j + w],
                           start=(j == 0), stop=(j == CONF_K - 1)) for j in range(CONF_K)]
                ph.op("tensor", fconv, reads=[ui, dg], writes=[pc])
                ph.op("vector", I("tensor_scalar", out=co.t[:, 0:w], in0=pc.t[:, 0:w], scalar1=cv.t[:, c:c + 1], scalar2=None, op0=ALU.add),
                      reads=[pc, cv], writes=[co])
                s = sq[c % 3]
                ph.op("scalar", I("activation", out=s.t[:, 0:w], in_=co.t[:, 0:w], func=AF.Square), reads=[co], writes=[s])
                pend.append((c, co, s))
                if len(pend) > 1:
                    emit_stats(*pend.pop(0), w)
            while pend:
                emit_stats(*pend.pop(0), w)
            ph.op("scalar", I("activation", out=mean.t[:, 0:w], in_=psum_s.t[:, 0:w], func=AF.Copy, scale=1.0 / CW), reads=[psum_s], writes=[mean])
            s = sq[0]
            ph.op("vector", I("tensor_tensor", out=s.t[:, 0:w], in0=mean.t[:, 0:w], in1=mean.t[:, 0:w], op=ALU.mult), reads=[mean], writes=[s])
            ph.op("vector", I("scalar_tensor_tensor", out=rstd.t[:, 0:w], in0=psum_q.t[:, 0:w], scalar=1.0 / CW, in1=s.t[:, 0:w],
                                                                op0=ALU.mult, op1=ALU.subtract), reads=[psum_q, s], writes=[rstd])
            ph.op("scalar", I("activation", out=rstd.t[:, 0:w], in_=rstd.t[:, 0:w], func=AF.Sqrt, bias=epsb.t[:, 0:1]), reads=[rstd, epsb], writes=[rstd])
            ph.op("vector", I("reciprocal", out=rstd.t[:, 0:w], in_=rstd.t[:, 0:w]), reads=[rstd], writes=[rstd])
            for c in range(CC):
                co = cout[c]
                o = ob[c % 3]
                ph.op("vector", I("tensor_tensor", out=co.t[:, 0:w], in0=co.t[:, 0:w], in1=mean.t[:, 0:w], op=ALU.subtract), reads=[co, mean], writes=[co])
                ph.op("vector", I("tensor_tensor", out=co.t[:, 0:w], in0=co.t[:, 0:w], in1=rstd.t[:, 0:w], op=ALU.mult), reads=[co, rstd], writes=[co])
                ph.op("scalar", I("activation", out=o.t[:, 0:w], in_=co.t[:, 0:w], func=AF.Silu,
                                                                       scale=cv.t[:, CC + c:CC + c + 1], bias=cv.t[:, 2 * CC + c:2 * CC + c + 1]),
                      reads=[co, cv], writes=[o])
                ph.store(UA[c * 128:(c + 1) * 128, tok0:tok0 + w], o.t[:, 0:w], reads=[o])
        ph.run()

    def phase_attention(name, lam_init, bg=()):
        if skip():
            return
        ph = Phase(nc, name)
        ph.add_bg(bg)
        NKT = cfg.NKT
        scale = 128 ** -0.5
        ident32 = ph.sbuf("id32", [128, 128], F32)
        identb = ph.sbuf("idb", [128, 128], BF16)
        ph.load(ident32.t[:], identd[:, :], writes=[ident32])
        ph.op("vector", I("tensor_copy", out=identb.t[:], in_=ident32.t[:]), reads=[ident32], writes=[identb])
        epsb = ph.sbuf("epsb", [128, 1], F32)
        ph.op("vector", I("memset", ap=epsb.t[:], constant=EPS), writes=[epsb])
        lv = ph.sbuf("lv", [128, 512], F32)
        ph.load(lv.t[:], lamv[:, :], writes=[lv])
        lt = ph.sbuf("lt", [128, 256], F32)
        l2 = ph.sbuf("l2", [128, 2], F32)
        neglam = ph.sbuf("neglam", [128, 1], F32)
        ph.op("vector", I("tensor_tensor", out=lt.t[:, 0:128], in0=lv.t[:, 0:128], in1=lv.t[:, 128:256], op=ALU.mult), reads=[lv], writes=[lt])
        ph.op("vector", I("tensor_tensor", out=lt.t[:, 128:256], in0=lv.t[:, 256:384], in1=lv.t[:, 384:512], op=ALU.mult), reads=[lv, lt], writes=[lt])
        ph.op("vector", I("reduce_sum", out=l2.t[:, 0:1], in_=lt.t[:, 0:128], axis=mybir.AxisListType.X), reads=[lt], writes=[l2])
        ph.op("vector", I("reduce_sum", out=l2.t[:, 1:2], in_=lt.t[:, 128:256], axis=mybir.AxisListType.X), reads=[lt, l2], writes=[l2])
        ph.op("scalar", I("activation", out=l2.t[:], in_=l2.t[:], func=AF.Exp), reads=[l2], writes=[l2])
        ph.op("vector", I("scalar_tensor_tensor", out=neglam.t[:], in0=l2.t[:, 1:2], scalar=-lam_init, in1=l2.t[:, 0:1],
                                                          op0=ALU.add, op1=ALU.subtract), reads=[l2], writes=[neglam])
        sgp = ph.sbuf("sgp", [128, 256], F32)
        ph.load(sgp.t[:], subg[:, :], writes=[sgp])
        cft = ph.sbuf("cft", [128, NQB * NKT * NH], F32)
        nmt = ph.sbuf("nmt", [128, NKT], F32)
        fct = ph.sbuf("fct", [128, NKT * NH], F32)
        ph.load(cft.t[:], cfar[:, :], writes=[cft])
        ph.load(nmt.t[:], nmd[:, :], writes=[nmt])
        ph.load(fct.t[:], fcnd[:, :], writes=[fct])
        ktb2 = [ph.sbuf(f"ktb{i}", [128, 2, S], BF16) for i in range(2)]
        vtb2 = [ph.sbuf(f"vtb{i}", [128, NKT, 257], BF16) for i in range(2)]
        qtb2 = [ph.sbuf(f"qtb{i}", [128, 2, TE], BF16) for i in range(2)]
        wtp2 = [ph.sbuf(f"wtp{i}", [128, wtoep_w], F32) for i in range(2)]

        def head_loads(h):
            ktb, vtb, qtb, wtp = ktb2[h % 2], vtb2[h % 2], qtb2[h % 2], wtp2[h % 2]
            for m in range(2):
                ph.load(ktb.t[:, m, :], KT[(h * 2 + m) * 128:(h * 2 + m + 1) * 128, :], writes=[ktb])
                ph.load(qtb.t[:, m, :], QT[(h * 2 + m) * 128:(h * 2 + m + 1) * 128, :], writes=[qtb])
            for k0 in range(0, NKT, 16):
                n = min(16, NKT - k0)
                ph.load(vtb.t[:, k0:k0 + n, 0:256],
                        VV[k0 * 128:(k0 + n) * 128, h * 256:(h + 1) * 256].rearrange("(k p) e -> p k e", p=128), writes=[vtb])
            ph.op("vector", I("memset", ap=vtb.t[:, :, 256:257], constant=1.0), writes=[vtb])
            ph.load(wtp.t[:], wtoep[h * 128:(h + 1) * 128, :], writes=[wtp])
        psS = [ph.psum(f"psS{i}", [128, 512], F32) for i in range(3)]
        pacc = [ph.psum(f"pacc{i}", [128, 512], F32) for i in range(4)]
        ptr = ph.psum("ptr", [128, 256], BF16)
        pt = [ph.sbuf(f"pt{i}", [128, 512], BF16) for i in range(4)]
        bt = [ph.sbuf(f"bt{i}", [128, 512], F32) for i in range(2)]
        tmp = [ph.sbuf(f"tmp{i}", [128, 512], F32) for i in range(2)]
        om = [[ph.sbuf(f"om{m}_{s}", [128, 257], F32) for s in range(4)] for m in range(2)]
        rr = [ph.sbuf(f"rr{i}", [128, 4], F32) for i in range(2)]
        oc = [ph.sbuf(f"oc{i}", [128, 256], F32) for i in range(2)]
        ocb = [ph.sbuf(f"ocb{i}", [128, 256], BF16) for i in range(2)]
        junk = ph.sbuf("junk", [128, 256], BF16)
        ath = [ph.sbuf("ath0", [128, 2, TE], BF16)] * 2
        cnt = {"s": 0, "p": 0, "b": 0, "o": 0}
        gain = 1.0 - lam_init
        head_loads(0)
        for h in range(NH):
            if h + 1 < NH:
                head_loads(h + 1)
            ktb, vtb, qtb, wtp = ktb2[h % 2], vtb2[h % 2], qtb2[h % 2], wtp2[h % 2]
            units = [(qb, qs, qw, m, j) for qb, (qs, qw) in enumerate(cfg.tblocks) for m in range(2) for j in range(NKT)]
            pbuf = {}
            LA = 2

            def emit_qk(u):
                qb, qs, qw, m, j = units[u]
                ps = psS[u % 3]
                ph.op("tensor", I("matmul", out=ps.t[:, 0:qw], lhsT=ktb.t[:, m, j * 128:(j + 1) * 128], rhs=qtb.t[:, m, qs:qs + qw],
                                  start=True, stop=True), reads=[ktb, qtb], writes=[ps])
                p = pt[u % 4]
                pbuf[u] = p
                isnear, dj = cfg.near(qs, qw, j)
                if isnear:
                    c0 = 576 - dj
                    b = bt[cnt["b"] % 2]
                    tm = tmp[cnt["b"] % 2]
                    cnt["b"] += 1
                    ph.op("vector", I("tensor_scalar", out=b.t[:, 0:qw], in0=wtp.t[:, c0:c0 + qw], scalar1=nmt.t[:, j:j + 1],
                                      scalar2=fct.t[:, j * NH + h:j * NH + h + 1], op0=ALU.mult, op1=ALU.add),
                          reads=[wtp, nmt, fct], writes=[b])
                    ph.op("vector", I("scalar_tensor_tensor", out=tm.t[:, 0:qw], in0=ps.t[:, 0:qw], scalar=scale, in1=b.t[:, 0:qw],
                                      op0=ALU.mult, op1=ALU.add), reads=[ps, b], writes=[tm])
                    ph.op("scalar", I("activation", out=p.t[:, 0:qw], in_=tm.t[:, 0:qw], func=AF.Exp), reads=[tm], writes=[p])
                else:
                    ci = (qb * NKT + j) * NH + h
                    ph.op("scalar", I("activation", out=p.t[:, 0:qw], in_=ps.t[:, 0:qw], func=AF.Exp, scale=scale,
                                      bias=cft.t[:, ci:ci + 1]), reads=[ps, cft], writes=[p])

            def emit_pv(u):
                qb, qs, qw, m, j = units[u]
                nsub = qw // 128
                p = pbuf.pop(u)
                fpv = [I("matmul", out=pacc[s].t[:, 0:257], lhsT=p.t[:, s * 128:(s + 1) * 128], rhs=vtb.t[:, j, :],
                         start=(j == 0), stop=(j == NKT - 1)) for s in range(nsub)]
                ph.op("tensor", fpv, reads=[p, vtb], writes=pacc[0:nsub])
                if j != NKT - 1:
                    return
                for s in range(nsub):
                    o = om[m][s]
                    ph.op("vector", I("tensor_copy", out=o.t[:], in_=pacc[s].t[:, 0:257]), reads=[pacc[s]], writes=[o])
                if m != 1:
                    return
                for s in range(nsub):
                    i = cnt["o"] % 2
                    cnt["o"] += 1
                    r, o, obf, ab = rr[i], oc[i], ocb[i], ath[h % 2]
                    o0, o1 = om[0][s], om[1][s]
                    ph.op("vector", I("reciprocal", out=r.t[:, 0:1], in_=o0.t[:, 256:257]), reads=[o0], writes=[r])
                    ph.op("vector", I("reciprocal", out=r.t[:, 1:2], in_=o1.t[:, 256:257]), reads=[o1, r], writes=[r])
                    ph.op("vector", I("tensor_tensor", out=r.t[:, 1:2], in0=r.t[:, 1:2], in1=neglam.t[:, 0:1], op=ALU.mult), reads=[r, neglam], writes=[r])
                    ph.op("vector", I("tensor_scalar", out=o1.t[:, 0:256], in0=o1.t[:, 0:256], scalar1=r.t[:, 1:2], scalar2=None, op0=ALU.mult),
                          reads=[o1, r], writes=[o1])
                    ph.op("vector", I("scalar_tensor_tensor", out=o.t[:], in0=o0.t[:, 0:256], scalar=r.t[:, 0:1], in1=o1.t[:, 0:256],
                                      op0=ALU.mult, op1=ALU.add), reads=[o0, o1, r], writes=[o])
                    ph.op("scalar", I("activation", out=junk.t[:], in_=o.t[:], func=AF.Square, accum_out=r.t[:, 2:3]), reads=[o, r], writes=[junk, r])
                    ph.op("scalar", I("activation", out=r.t[:, 2:3], in_=r.t[:, 2:3], func=AF.Sqrt, scale=1.0 / 256, bias=epsb.t[:, 0:1]), reads=[r, epsb], writes=[r])
                    ph.op("vector", I("reciprocal", out=r.t[:, 3:4], in_=r.t[:, 2:3]), reads=[r], writes=[r])
                    ph.op("vector", I("tensor_scalar", out=r.t[:, 3:4], in0=r.t[:, 3:4], scalar1=gain, scalar2=None, op0=ALU.mult), reads=[r], writes=[r])
                    ph.op("vector", I("scalar_tensor_tensor", out=obf.t[:], in0=o.t[:], scalar=r.t[:, 3:4], in1=sgp.t[:],
                                      op0=ALU.mult, op1=ALU.mult), reads=[o, r, sgp], writes=[obf])
                    ftr = [I("transpose", out=ptr.t[:, 0:128], in_=obf.t[:, 0:128], identity=identb.t[:]),
                           I("transpose", out=ptr.t[:, 128:256], in_=obf.t[:, 128:256], identity=identb.t[:])]
                    ph.op("tensor", ftr, reads=[obf, identb], writes=[ptr])
                    t0 = qs + s * 128
                    copy_op(ph, "scalar", ab.t[:, :, t0:t0 + 128], ptr.t[:].rearrange("p (a k) -> p a k", k=128), [ptr], [ab])

            for idx in range(len(units) + LA):
                if idx < len(units):
                    emit_qk(idx)
                if idx - LA >= 0:
                    emit_pv(idx - LA)
            if ATT_CUT[0] >= 6:
                ph.store(AT[h * 256:h * 256 + 128, :], ath[h % 2].t[:, 0, :], reads=[ath[h % 2]])
                ph.store(AT[h * 256 + 128:h * 256 + 256, :], ath[h % 2].t[:, 1, :], reads=[ath[h % 2]])
        ph.run()

    def phase_ffn_elt(name, layer):
        if skip():
            return
        ph = Phase(nc, name)
        cw = ph.sbuf("cw", [128, 2 * FC * 3], F32)
        cb = ph.sbuf("cb", [128, 2 * FC], F32)
        ph.load(cw.t[:], ffn_cw[:, :], writes=[cw])
        ph.load(cb.t[:], ffn_cb[:, :], writes=[cb])
        gin = [ph.sbuf(f"gin{i}", [128, TE + 2], F32) for i in range(2)]
        upb = [ph.sbuf(f"up{i}", [128, TE], BF16) for i in range(2)]
        acc = [ph.sbuf(f"acc{i}", [128, TE], F32) for i in range(2)]
        hb = [ph.sbuf(f"hb{i}", [128, TE], BF16) for i in range(2)]
        for i in range(2):
            ph.op("vector", I("memset", ap=gin[i].t[:], constant=0.0), writes=[gin[i]])
        for c in range(FC):
            g, u, a, hh = gin[c % 2], upb[c % 2], acc[c % 2], hb[c % 2]
            ph.load(g.t[:, 1:TE + 1], GG[c * 128:(c + 1) * 128, :], writes=[g])
            ph.load(u.t[:], UP[c * 128:(c + 1) * 128, :], writes=[u])
            wb = (layer * FC + c) * 3
            bb = layer * FC + c

            fconv = [I("tensor_scalar", out=a.t[:], in0=g.t[:, 0:TE], scalar1=cw.t[:, wb:wb + 1], scalar2=cb.t[:, bb:bb + 1], op0=ALU.mult, op1=ALU.add),
                     I("scalar_tensor_tensor", out=a.t[:], in0=g.t[:, 1:TE + 1], scalar=cw.t[:, wb + 1:wb + 2], in1=a.t[:], op0=ALU.mult, op1=ALU.add),
                     I("scalar_tensor_tensor", out=a.t[:], in0=g.t[:, 2:TE + 2], scalar=cw.t[:, wb + 2:wb + 3], in1=a.t[:], op0=ALU.mult, op1=ALU.add)]
            ph.op("vector", fconv, reads=[g, cw, cb], writes=[a])
            ph.op("scalar", I("activation", out=a.t[:], in_=a.t[:], func=AF.Gelu_apprx_tanh), reads=[a], writes=[a])
            ph.op("vector", I("tensor_tensor", out=hh.t[:], in0=a.t[:], in1=u.t[:], op=ALU.mult), reads=[a, u], writes=[hh])
            ph.store(HT[c * 128:(c + 1) * 128, :], hh.t[:], reads=[hh])
        ph.run()

    def phase_odd_elt(name):
        if skip():
            return
        ph = Phase(nc, name)
        cw = ph.sbuf("cw", [128, KC * 3], F32)
        ph.load(cw.t[:], od_cw[:, :], writes=[cw])
        pin = [ph.sbuf(f"pin{i}", [128, TE + 2], F32) for i in range(2)]
        gb = [ph.sbuf(f"gb{i}", [128, TE], BF16) for i in range(2)]
        acc = [ph.sbuf(f"acc{i}", [128, TE], F32) for i in range(2)]
        yb = [ph.sbuf(f"yb{i}", [128, TE], BF16) for i in range(2)]
        for i in range(2):
            ph.op("vector", I("memset", ap=pin[i].t[:], constant=0.0), writes=[pin[i]])
        for c in range(KC):
            g, u, a, y = pin[c % 2], gb[c % 2], acc[c % 2], yb[c % 2]
            ph.load(g.t[:, 1:TE + 1], PP[c * 128:(c + 1) * 128, :], writes=[g])
            ph.load(u.t[:], GB[c * 128:(c + 1) * 128, :], writes=[u])
            wb = c * 3

            fconv = [I("tensor_scalar", out=a.t[:], in0=g.t[:, 0:TE], scalar1=cw.t[:, wb:wb + 1], scalar2=None, op0=ALU.mult),
                     I("scalar_tensor_tensor", out=a.t[:], in0=g.t[:, 1:TE + 1], scalar=cw.t[:, wb + 1:wb + 2], in1=a.t[:], op0=ALU.mult, op1=ALU.add),
                     I("scalar_tensor_tensor", out=a.t[:], in0=g.t[:, 2:TE + 2], scalar=cw.t[:, wb + 2:wb + 3], in1=a.t[:], op0=ALU.mult, op1=ALU.add)]
            ph.op("vector", fconv, reads=[g, cw], writes=[a])
            ph.op("vector", I("tensor_tensor", out=y.t[:], in0=a.t[:], in1=u.t[:], op=ALU.mult), reads=[a, u], writes=[y])
            ph.store(YT[c * 128:(c + 1) * 128, :], y.t[:], reads=[y])
        ph.run()

    qkp = cfg.QK // 256
    avp = cfg.AW // 256
    cwp = CW // 256
    fp = DFF // 256
    dp = D // 256
    kvp = list(range(qkp, 2 * qkp + avp))
    restp = [p_ for p_ in range(ws_ev_in.npan) if p_ not in kvp]
    phase_bgonly("wc0", conv_items(w_ev_in, 0, ws_ev_in, kvp))
    bg_kv = (conv_items(w_ev_in, 0, ws_ev_in, restp) + conv_items(w_ev_out, 0, ws_ev_out)
             + conv_items(w_gate, 0, ws_gate[0]) + conv_items(w_up, 0, ws_up[0]))
    bg_qc = conv_items(w_down, 0, ws_down[0])
    bg_conf = conv_items(w_od_in, 0, ws_od_in)
    bg_attn = (conv_items(w_od_out, 0, ws_od_out) + conv_items(w_gate, D, ws_gate[1])
               + conv_items(w_up, D, ws_up[1]) + conv_items(w_down, DFF, ws_down[1]))
    phase_inproj("kv", xseq, cfg.sblocks, 0, ws_ev_in, [
        {"kind": "single", "panels": list(range(qkp, 2 * qkp)), "out": KT, "dt": BF16},
        {"kind": "tok", "panels": list(range(2 * qkp, 2 * qkp + avp)), "out": VV},
    ], bg=bg_kv)
    phase_inproj("qc", xown, cfg.tblocks, 0, ws_ev_in, [
        {"kind": "single", "panels": list(range(0, qkp)), "out": QT, "dt": BF16},
        {"kind": "glu", "panels_a": list(range(2 * qkp + avp, 2 * qkp + avp + cwp)),
         "panels_b": list(range(2 * qkp + avp + cwp, 2 * qkp + avp + 2 * cwp)), "out": UU},
    ], bg=bg_qc)
    phase_conformer("conf", bg=bg_conf)
    phase_attention("attn", 0.8 - 0.6 * math.exp(-0.3 * 0), bg=bg_attn)
    phase_outproj("o0", [(AT, cfg.AW // 128), (UA, CW // 128)], ws_ev_out)

    def ffn(layer, xprev, xin, xout, final=False):
        phase_inproj(f"fg{layer}", xin, cfg.tblocks, layer * 4 + 2, ws_gate[layer], [
            {"kind": "single", "panels": list(range(fp)), "out": GG, "dt": F32}],
            resid={"g_idx": layer * 4 + 1, "xin": xprev, "xout": xin})
        phase_inproj(f"fu{layer}", xin, cfg.tblocks, layer * 4 + 2, ws_up[layer], [
            {"kind": "ffn_up", "panels": list(range(fp)), "out": HT, "layer": layer}])
        phase_outproj(f"fd{layer}", [(HT, FC)], ws_down[layer])
        if final:
            phase_resid(f"fr{layer}", layer * 4 + 3, xin, xout, final=True)

    ffn(0, xown, X1, X2)
    phase_inproj("od", X2, cfg.tblocks, 4, ws_od_in, [
        {"kind": "single", "panels": list(range(0, dp)), "out": GB, "dt": BF16},
        {"kind": "mul", "panels_a": list(range(dp, 2 * dp)), "panels_b": list(range(2 * dp, 3 * dp)), "out": PP},
    ], resid={"g_idx": 3, "xin": X1, "xout": X2})
    phase_odd_elt("oe")
    phase_outproj("o1", [(YT, KC)], ws_od_out)
    ffn(1, X2, X3, yout, final=True)
    return nc


def host_inputs(cfg, inp):
    D, S, DFF, NH, TE, NT, NKT = cfg.D, cfg.S, cfg.DFF, cfg.NH, cfg.TE, cfg.NT, cfg.NKT
    CC = cfg.CW // 128
    FC = DFF // 128
    KC = cfg.KC
    f = lambda a: np.ascontiguousarray(np.asarray(a, dtype=np.float32))
    x = f(inp["x"])
    rel_bias = f(inp["rel_bias"])
    rep = lambda v: np.broadcast_to(f(v)[None, :], (128, f(v).shape[0]))
    shared = {
        "ev_w_in": f(inp["ev_w_in"])[0], "ev_w_out": f(inp["ev_w_out"])[0],
        "od_w_in": f(inp["od_w_in"])[0], "od_w_out": f(inp["od_w_out"])[0],
        "ffn_w_gate": f(inp["ffn_w_gate"]).reshape(2 * D, DFF),
        "ffn_w_up": f(inp["ffn_w_up"]).reshape(2 * D, DFF),
        "ffn_w_down": f(inp["ffn_w_down"]).reshape(2 * DFF, D),
    }
    gl = []
    for l in range(2):
        for nm in ("pre_mix_g", "post_mix_g", "pre_ffn_g", "post_ffn_g"):
            gl.append(rep(f(inp[nm])[l]))
    shared["grep"] = np.ascontiguousarray(np.concatenate(gl, 0))
    chunked = lambda v: np.ascontiguousarray(f(v).reshape(-1, 128).T)
    cw = f(inp["ev_conf_w"])[0]
    shared["conf_w"] = np.ascontiguousarray(cw.T.reshape(CC, 128, CONF_K).transpose(1, 0, 2).reshape(128, CC * CONF_K))
    shared["conf_v"] = np.ascontiguousarray(np.concatenate(
        [chunked(f(inp["ev_conf_b"])[0]), chunked(f(inp["ev_conf_ln_g"])[0]), chunked(f(inp["ev_conf_ln_b"])[0])], 1))
    ow = f(inp["od_conv_w"])[0]
    shared["od_cw"] = np.ascontiguousarray(ow.T.reshape(KC, 128, 3).transpose(1, 0, 2).reshape(128, KC * 3))
    fw_ = f(inp["ffn_conv_w"])
    shared["ffn_cw"] = np.ascontiguousarray(fw_.transpose(0, 2, 1).reshape(2, FC, 128, 3).transpose(2, 0, 1, 3).reshape(128, 2 * FC * 3))
    fb = f(inp["ffn_conv_b"])
    shared["ffn_cb"] = np.ascontiguousarray(fb.reshape(2, FC, 128).transpose(2, 0, 1).reshape(128, 2 * FC))
    shared["subg"] = np.ascontiguousarray(rep(f(inp["ev_subln_g"])[0]))
    shared["lamv"] = np.ascontiguousarray(np.concatenate(
        [rep(f(inp[k])[0]) for k in ("ev_lambda_q1", "ev_lambda_k1", "ev_lambda_q2", "ev_lambda_k2")], 1))
    ii = np.arange(128)[:, None]
    cc = np.arange(512 + 768)[None, :]
    idx = t5_bucket_np(ii - cc + 576)
    shared["wtoep"] = np.ascontiguousarray(np.concatenate([rel_bias[idx, h] for h in range(NH)], 0))
    shared["ident"] = np.eye(128, dtype=np.float32)
    left = rel_bias[t5_bucket_np(np.array(-100000)), :]
    right = rel_bias[t5_bucket_np(np.array(100000)), :]
    maps = []
    NQB = len(cfg.tblocks)
    for c in range(8):
        b, q = c // 4, c % 4
        a = q * cfg.TOWN
        xo = np.zeros((TE, D), np.float32)
        lo, hi = a - 64, a + cfg.TOWN + 64
        slo, shi = max(lo, 0), min(hi, S)
        xo[slo - lo:shi - lo] = x[b, slo:shi]
        msk = np.zeros((TE,), np.float32)
        msk[slo - lo:shi - lo] = 1.0
        rot = a - 256
        xs = np.roll(x[b], -rot, axis=0)
        tabs = np.arange(NKT) + rot // 128
        wrapped = (tabs < 0) | (tabs >= NKT)
        nm = np.where(wrapped, 0.0, 1.0).astype(np.float32)
        fcn = np.zeros((NKT, NH), np.float32)
        fcn[tabs < 0] = right
        fcn[tabs >= NKT] = left
        cf = np.zeros((NQB, NKT, NH), np.float32)
        for qb, (qs, qw) in enumerate(cfg.tblocks):
            for j in range(NKT):
                if tabs[j] < 0:
                    cf[qb, j] = right
                elif tabs[j] >= NKT:
                    cf[qb, j] = left
                else:
                    dj = 128 * j - cfg.QOFF - qs
                    cf[qb, j] = left if dj < 0 else right
        m = dict(shared)
        m["xown"] = xo
        m["xseq"] = np.ascontiguousarray(xs)
        m["mask"] = np.ascontiguousarray(msk.reshape(NT, 128).T)
        m["cfar"] = np.ascontiguousarray(np.broadcast_to(cf.reshape(1, -1), (128, NQB * NKT * NH)))
        m["nm"] = np.ascontiguousarray(np.broadcast_to(nm.reshape(1, -1), (128, NKT)))
        m["fcn"] = np.ascontiguousarray(np.broadcast_to(fcn.reshape(1, -1), (128, NKT * NH)))
        maps.append(m)
    return maps


def run(cfg, inp, dbg=()):
    nc = build_program(cfg, dbg)
    maps = host_inputs(cfg, inp)
    res = run_bass_kernel_spmd(nc, maps, core_ids=list(range(8)))
    out = np.zeros((2, cfg.S, cfg.D), np.float32)
    for c in range(8):
        b, q = c // 4, c % 4
        out[b, q * cfg.TOWN:(q + 1) * cfg.TOWN] = res.results[c]["y"]
    return out, res


def kernel(**inputs):
    cfg = Cfg(4096, 8192, 11008)
    out, _ = run(cfg, inputs)
    return out
```
